# Optimizing a Trainium2 kernel written in Bass

```python
import math
import jax
import jax.numpy as jnp
from jax import lax
import numpy as np

D_MODEL = 2048
BATCH = 16
SEQ = 256
DEPTH = 2
DEC_BATCH = 4
DEC_SEQ = 2048
PAST_LEN = 256

GRID_W = 64
EPS = 1e-6
HEAD_DIM = 128
A_HEADS = D_MODEL // 256
A_KV_HEADS = A_HEADS // 4
A_WIDTH = A_HEADS * HEAD_DIM
KV_WIDTH = A_KV_HEADS * HEAD_DIM
Q_BLOCK = 128
ROPE_THETA = 10000.0
ATTN_SCALE = HEAD_DIM ** -0.5
CHUNK = 128
B_GROUPS = 4
B_WIDTH = D_MODEL // 4
B_GC = B_WIDTH // B_GROUPS
POOL_WINDOWS = (2, 4, 8, 16)
C_GROUPS = 4
C_WIDTH = D_MODEL // 4
C_GC = C_WIDTH // C_GROUPS
N_BRANCH = 3
IN_SPLITS = (A_WIDTH, KV_WIDTH, KV_WIDTH, A_WIDTH, B_WIDTH, B_WIDTH, B_WIDTH, C_WIDTH, C_WIDTH)
IN_COLS = 2 * A_WIDTH + 2 * KV_WIDTH + 3 * B_WIDTH + 2 * C_WIDTH

kernel_name = "hybrid_ctx_prefix_gqa_sgu_pool_step"


def _rmsnorm(x, w):
    xf = x.astype(jnp.float32)
    y = xf * lax.rsqrt(jnp.mean(xf * xf, axis=-1, keepdims=True) + EPS)
    return (y * w.astype(jnp.float32)).astype(x.dtype)


def _split_cols(proj):
    idx = []
    acc = 0
    for s in IN_SPLITS[:-1]:
        acc += s
        idx.append(acc)
    return jnp.split(proj, idx, axis=-1)


def _axial_rope_tables(n_tokens):
    rows = n_tokens // GRID_W
    row = jnp.repeat(jnp.arange(rows), GRID_W).astype(jnp.float32)
    col = jnp.tile(jnp.arange(GRID_W), rows).astype(jnp.float32)
    n_freq = HEAD_DIM // 4
    inv = ROPE_THETA ** (-jnp.arange(n_freq, dtype=jnp.float32) / n_freq)
    ang_r = row[:, None] * inv[None, :]
    ang_c = col[:, None] * inv[None, :]
    return (jnp.cos(ang_r), jnp.sin(ang_r), jnp.cos(ang_c), jnp.sin(ang_c))


def _rotate(xh, cos, sin):
    f = xh.shape[-1] // 2
    x1, x2 = xh[..., :f], xh[..., f:]
    cos = cos[None, :, None, :]
    sin = sin[None, :, None, :]
    return jnp.concatenate([x1 * cos - x2 * sin, x2 * cos + x1 * sin], axis=-1)


def _apply_axial_rope(x, tables):
    cr, sr, cc, sc = tables
    xf = x.astype(jnp.float32)
    half = HEAD_DIM // 2
    out = jnp.concatenate([_rotate(xf[..., :half], cr, sr), _rotate(xf[..., half:], cc, sc)], axis=-1)
    return out.astype(x.dtype)


def _block_attention(q, k, v):
    b, s = q.shape[0], q.shape[1]
    g = A_HEADS // A_KV_HEADS
    nb = s // Q_BLOCK
    qb = q.reshape(b, nb, Q_BLOCK, A_KV_HEADS, g, HEAD_DIM).transpose(1, 0, 2, 3, 4, 5)
    kf = k.astype(jnp.float32)
    vf = v.astype(jnp.float32)

    def one_block(qblk):
        sc = jnp.einsum('bqkgd,btkd->bkgqt', qblk.astype(jnp.float32), kf) * ATTN_SCALE
        p = jax.nn.softmax(sc, axis=-1)
        return jnp.einsum('bkgqt,btkd->bqkgd', p, vf).astype(q.dtype)

    o = lax.map(one_block, qb)
    return o.transpose(1, 0, 2, 3, 4, 5).reshape(b, s, A_WIDTH)


def _chunk_sgu(u, v, norm_w, w_s, b_s):
    b, s, _ = v.shape
    vc = _rmsnorm(v, norm_w).reshape(b, s // CHUNK, CHUNK, B_GROUPS, B_GC)
    mixed = jnp.einsum('gpq,bnqgc->bnpgc', w_s, vc) + b_s.T[None, None, :, :, None]
    return u * mixed.reshape(b, s, B_WIDTH)


def _multiscale_pool(z, w_pool, pool_scale):
    b, s, _ = z.shape
    zf = z.astype(jnp.float32)
    cs = jnp.concatenate([jnp.zeros((b, 1, C_WIDTH), jnp.float32), jnp.cumsum(zf, axis=1)], axis=1)
    t = jnp.arange(s)
    outs = []
    for gi, w in enumerate(POOL_WINDOWS):
        lo = jnp.clip(t - w // 2, 0, s)
        hi = jnp.clip(t + w - w // 2, 0, s)
        sl = slice(gi * C_GC, (gi + 1) * C_GC)
        cnt = (hi - lo).astype(jnp.float32)[None, :, None]
        d = (cs[:, hi, sl] - cs[:, lo, sl]) / cnt - zf[..., sl]
        outs.append(jnp.einsum('bsc,cd->bsd', d, w_pool[gi].astype(jnp.float32)))
    out = jnp.concatenate(outs, axis=-1) * pool_scale.astype(jnp.float32)
    return out.astype(z.dtype)


def _mixer_layer(x, mod, rope_tables, ctx_k, ctx_v, norm_w, w_in, q_norm_w, k_norm_w,
                 sgu_norm_w, w_sgu, b_sgu, w_pool, pool_scale, w_br_a, w_br_b, w_br_c,
                 w_merge, b_merge, w_out):
    b, s, _ = x.shape
    shift, scale, gate = jnp.split(mod.astype(x.dtype), 3, axis=-1)
    h = _rmsnorm(x, norm_w) * (1.0 + scale) + shift
    q, k, v, ga, u, vb, gb, z, gc = _split_cols(h @ w_in)
    q = _rmsnorm(q.reshape(b, s, A_HEADS, HEAD_DIM), q_norm_w)
    k = _rmsnorm(k.reshape(b, s, A_KV_HEADS, HEAD_DIM), k_norm_w)
    v = v.reshape(b, s, A_KV_HEADS, HEAD_DIM)
    k_plain = k
    if rope_tables is None:
        keys, vals = k, v
    else:
        q = _apply_axial_rope(q, rope_tables)
        keys = jnp.concatenate([_apply_axial_rope(k, rope_tables), ctx_k.astype(x.dtype)], axis=1)
        vals = jnp.concatenate([v, ctx_v.astype(x.dtype)], axis=1)
    attn = _block_attention(q, keys, vals) * jax.nn.silu(ga)
    bout = _chunk_sgu(u, vb, sgu_norm_w, w_sgu, b_sgu) * jax.nn.silu(gb)
    cout = _multiscale_pool(z, w_pool, pool_scale) * jax.nn.silu(gc)
    gates = jax.nn.sigmoid((h @ w_merge + b_merge).astype(jnp.float32)).astype(x.dtype)
    gates = gates.reshape(b, s, N_BRANCH, D_MODEL)
    merged = (gates[:, :, 0] * (attn @ w_br_a)
              + gates[:, :, 1] * (bout @ w_br_b)
              + gates[:, :, 2] * (cout @ w_br_c))
    out = merged @ w_out
    return x + gate * out, k_plain, v


def setup_inputs(seed: int = 0) -> dict:
    key = jax.random.key(seed)
    ks = jax.random.split(key, 32)
    nrm = jax.random.normal
    f32 = jnp.float32
    d = D_MODEL
    return {
        "x_prompt": nrm(ks[0], (BATCH, SEQ, d), f32),
        "x_sample": nrm(ks[1], (DEC_BATCH, DEC_SEQ, d), f32),
        "cache_k": nrm(ks[2], (DEC_BATCH, DEPTH, PAST_LEN, A_KV_HEADS, HEAD_DIM), f32),
        "cache_v": nrm(ks[3], (DEC_BATCH, DEPTH, PAST_LEN, A_KV_HEADS, HEAD_DIM), f32),
        "c": nrm(ks[4], (DEC_BATCH, d), f32),
        "c_ctx": nrm(ks[5], (d,), f32),
        "norm_w": 1.0 + 0.02 * nrm(ks[6], (DEPTH, d), f32),
        "w_ada": 0.5 * d ** -0.5 * nrm(ks[7], (DEPTH, d, 3 * d), f32),
        "b_ada": 0.01 * nrm(ks[8], (DEPTH, 3 * d), f32),
        "w_in": d ** -0.5 * nrm(ks[9], (DEPTH, d, IN_COLS), f32),
        "q_norm_w": 1.0 + 0.02 * nrm(ks[10], (DEPTH, HEAD_DIM), f32),
        "k_norm_w": 1.0 + 0.02 * nrm(ks[11], (DEPTH, HEAD_DIM), f32),
        "sgu_norm_w": 1.0 + 0.02 * nrm(ks[12], (DEPTH, B_WIDTH), f32),
        "w_sgu": CHUNK ** -0.5 * nrm(ks[13], (DEPTH, B_GROUPS, CHUNK, CHUNK), f32),
        "b_sgu": 0.02 * nrm(ks[14], (DEPTH, B_GROUPS, CHUNK), f32),
        "w_pool": C_GC ** -0.5 * nrm(ks[15], (DEPTH, C_GROUPS, C_GC, C_GC), f32),
        "pool_scale": 1.0 + 0.02 * nrm(ks[16], (DEPTH, C_WIDTH), f32),
        "w_br_a": A_WIDTH ** -0.5 * nrm(ks[17], (DEPTH, A_WIDTH, d), f32),
        "w_br_b": B_WIDTH ** -0.5 * nrm(ks[18], (DEPTH, B_WIDTH, d), f32),
        "w_br_c": C_WIDTH ** -0.5 * nrm(ks[19], (DEPTH, C_WIDTH, d), f32),
        "w_merge": d ** -0.5 * nrm(ks[20], (DEPTH, d, N_BRANCH * d), f32),
        "b_merge": 0.01 * nrm(ks[21], (DEPTH, N_BRANCH * d), f32),
        "w_out": d ** -0.5 * nrm(ks[22], (DEPTH, d, d), f32),
        "final_norm_w": 1.0 + 0.02 * nrm(ks[23], (d,), f32),
    }


def reference(x_prompt, x_sample, cache_k, cache_v, c, c_ctx, norm_w, w_ada, b_ada, w_in,
              q_norm_w, k_norm_w, sgu_norm_w, w_sgu, b_sgu, w_pool, pool_scale,
              w_br_a, w_br_b, w_br_c, w_merge, b_merge, w_out, final_norm_w):
    rope_tables = _axial_rope_tables(x_sample.shape[1])
    xp = x_prompt
    xs = x_sample
    ks_out = []
    vs_out = []
    for l in range(DEPTH):
        lp = (norm_w[l], w_in[l], q_norm_w[l], k_norm_w[l], sgu_norm_w[l], w_sgu[l], b_sgu[l],
              w_pool[l], pool_scale[l], w_br_a[l], w_br_b[l], w_br_c[l], w_merge[l], b_merge[l],
              w_out[l])
        mod_ctx = (jax.nn.silu(c_ctx) @ w_ada[l] + b_ada[l])[None, None, :]
        xp, k_l, v_l = _mixer_layer(xp, mod_ctx, None, None, None, *lp)
        ks_out.append(k_l)
        vs_out.append(v_l)
        mod_lat = (jax.nn.silu(c) @ w_ada[l] + b_ada[l])[:, None, :]
        xs, _, _ = _mixer_layer(xs, mod_lat, rope_tables, cache_k[:, l], cache_v[:, l], *lp)
    y_prompt = _rmsnorm(xp, final_norm_w)
    y_sample = _rmsnorm(xs, final_norm_w)
    state_k = jnp.stack(ks_out, axis=1)
    state_v = jnp.stack(vs_out, axis=1)
    return (y_prompt, y_sample, state_k, state_v)
```

```python
import numpy as np
import ml_dtypes
import concourse.bass as bass
import concourse.mybir as mybir
from concourse.bass_utils import run_bass_kernel_spmd

F32 = mybir.dt.float32
BF16 = mybir.dt.bfloat16
AF = mybir.ActivationFunctionType
ALU = mybir.AluOpType

D = 2048
T = 1536
NT = 12
KC = 16
NS = 1024
EPS = 1e-6
ATT_SCALE = 128 ** -0.5
NKEY = 2304
PADL = 16
XCH = 2048 + 2048 + 256


class Res:
    __slots__ = ("name", "w", "r", "excl")

    def __init__(self, name, excl=False):
        self.name = name
        self.w = None
        self.r = []
        self.excl = excl


class Op:
    __slots__ = ("eng", "fn", "deps", "sig", "semkey", "val", "inc", "isdma", "clock")


class Prog:
    ENG = ("pe", "act", "dve", "pool", "sp")

    def __init__(self):
        self.ops = {e: [] for e in self.ENG}
        self.dmacount = {}
        self.clock = 0

    def op(self, eng, fn, reads=(), writes=(), after=(), dma=None, inc=16):
        o = Op()
        o.clock = self.clock
        o.eng = eng
        o.fn = fn
        o.sig = False
        o.isdma = dma is not None
        deps = set()
        excl_reads = [r for r in reads if r.excl]
        reads = [r for r in reads if not r.excl]
        writes = list(writes) + excl_reads
        for r in reads:
            if r.w is not None:
                deps.add(r.w)
        for w in writes:
            if w.w is not None:
                deps.add(w.w)
            for rd in w.r:
                deps.add(rd)
        for a in after:
            if a is not None:
                deps.add(a)
        o.deps = deps
        for d in deps:
            d.sig = True
        for r in reads:
            r.r.append(o)
        for w in writes:
            w.w = o
            w.r = []
        if dma is not None:
            o.semkey = dma
            self.dmacount[dma] = self.dmacount.get(dma, 0) + 1
            o.val = self.dmacount[dma] * inc
            o.inc = inc
            o.sig = True
        else:
            o.semkey = eng
            o.val = None
            o.inc = 1
        self.ops[eng].append(o)
        return o

    def last(self, eng):
        for o in reversed(self.ops[eng]):
            if not o.isdma:
                return o
        return None

    def barrier(self):
        return [self.last(e) for e in ("pe", "act", "dve", "pool")]

    def emit(self, nc, sems, block):
        for eng in ("pe", "act", "dve", "pool"):
            cnt = 0
            for o in self.ops[eng]:
                if (not o.isdma) and o.sig:
                    cnt += 1
                    o.val = cnt
        finals = dict((k, v) for k, v in self.dmacount.items())

        def run(e, eng):
            waited = {}
            for o in self.ops[eng]:
                need = {}
                for d in o.deps:
                    if need.get(d.semkey, 0) < d.val:
                        need[d.semkey] = d.val
                for k, v in need.items():
                    if waited.get(k, 0) < v:
                        e.wait_ge(sems[k], v)
                        waited[k] = v
                ins = o.fn(e)
                if o.sig:
                    ins.then_inc(sems[o.semkey], o.inc)
            if eng == "sp":
                for o in self.ops["sp"] + self.ops["pool"] + self.ops["act"]:
                    if o.isdma and getattr(o, "final", False):
                        pass
                for k in sorted(self.out_keys):
                    e.wait_ge(sems[k], self.lastval[k])

        @block.tensor
        def _(e):
            run(e, "pe")

        @block.scalar
        def _(e):
            run(e, "act")

        @block.vector
        def _(e):
            run(e, "dve")

        @block.gpsimd
        def _(e):
            run(e, "pool")

        @block.sync
        def _(e):
            run(e, "sp")


def build_nc(groups=None, nlayers=2, stop=None):
    if groups is None:
        groups = [[0, 1], [2, 3], [4, 5], [6, 7]]
    nc = bass.Bass("TRN2", target_bir_lowering=False)
    _memo = {}
    import contextlib
    es = contextlib.ExitStack()

    def _mem(key, fn):
        if key not in _memo:
            _memo[key] = fn()
        return _memo[key]

    def plan(P, wplan, rec):

        def din(name, shape, dt=F32):
            return _mem("d_" + name, lambda: nc.dram_tensor(name, list(shape), dt, kind="ExternalInput").ap())

        def dout(name, shape):
            return _mem("d_" + name, lambda: nc.dram_tensor(name, list(shape), F32, kind="ExternalOutput").ap())

        xs = din("xs", [T, D])
        w_ada = din("w_ada", [2, D, 6144])
        w_in = din("w_in", [2, D, 5120])
        w_merge = din("w_merge", [2, D, 6144])
        w_br_a = din("w_br_a", [2, 1024, D])
        w_br_b = din("w_br_b", [2, 512, D])
        w_br_c = din("w_br_c", [2, 512, D])
        w_out = din("w_out", [2, D, D])
        w_pool = din("w_pool", [2, 4, 128, 128])
        ck = din("ck", [2, 256, 256])
        cv = din("cv", [2, 256, 256])
        vecs = din("vecs", [2, 128, 128])
        cT = din("cT", [128, 32])
        fnw = din("fnw", [128, D])
        sgn = din("sgn", [2, 128, 512])
        bsg = din("bsg", [2, 128, 512])
        wsT = din("wsT", [2, 128, 512])
        knw = din("knw", [2, 128, 128])
        ropec = din("ropec", [128, NS])
        ropes = din("ropes", [128, NS])
        invc = din("invc", [4, T])
        mlr = din("mlr", [128, 16])
        cmat = din("cmat", [128, 256])

        y = dout("y", [T, D])
        sk = dout("sk", [2, 512, 256])
        sv = dout("sv", [2, 512, 256])

        x1 = _mem("x1", lambda: nc.dram_tensor("x1", [T, D], F32).ap())
        x2 = _mem("x2", lambda: nc.dram_tensor("x2", [T, D], F32).ap())
        xch_in = [_mem("xi%d" % l, lambda l=l: nc.dram_tensor("xch_in%d" % l, [128, XCH], BF16).ap()) for l in range(2)]
        xch_out = [_mem("xo%d" % l, lambda l=l: nc.dram_tensor("xch_out%d" % l, [256, XCH], BF16).ap()) for l in range(2)]


        def sb(name, shape, dt):
            return _mem("s_" + name, lambda: es.enter_context(nc.sbuf_tensor(name, list(shape), dt)))

        hT = sb("hT", [128, KC * T], BF16)
        qT = sb("qT", [128, 8 * T], BF16)
        regA = sb("regA", [128, 12288], BF16)
        regM = sb("regM", [128, 12288], F32)
        regMb = regM[:].bitcast(BF16)
        wbuf = sb("wbuf", [128, 6 * 2048], BF16)
        gbc = sb("gbc", [128, 2 * D], BF16)
        cst = sb("cst", [128, 4 * 128], BF16)
        vec = sb("vec", [128, 128], F32)
        AB = sb("AB", [128, 128], F32)
        rc = sb("rc", [128, NS], BF16)
        rs = sb("rs", [128, NS], BF16)
        sgn_t = sb("sgn_t", [128, 512], F32)
        bsg_t = sb("bsg_t", [128, 512], F32)
        wsT_t = sb("wsT_t", [128, 512], BF16)
        knw_t = sb("knw_t", [128, 128], F32)
        wp_t = sb("wp_t", [128, 512], BF16)
        mlr_t = sb("mlr_t", [128, 16], F32)
        epsb = sb("epsb", [128, 1], F32)
        st = sb("st", [128, 16], F32)
        t6 = sb("t6", [128, 7 * 512], F32)
        t6b = sb("t6b", [128, 2 * 512], BF16)

        ps = [_mem("ps%d" % i, lambda i=i: es.enter_context(nc.psum_tensor("ps%d" % i, [128, 512], F32))) for i in range(8)]
        R_ps = [Res("ps%d" % i, excl=True) for i in range(8)]

        ident = cst[:, 0:128]
        perm = cst[:, 128:256]
        onesm = cst[:, 256:384]
        ones = cst[:, 384:512]

        R_h = [Res("h%d" % i) for i in range(NT)]
        R_q = [[Res("q%d_%d" % (h, c)) for c in range(3)] for h in range(8)]
        R_regA = Res("regA")
        R_kTs = Res("kTs")
        R_Vs = Res("Vs")
        R_kTp = [Res("kTp%d" % g) for g in range(2)]
        R_Vp = [Res("Vp%d" % i) for i in range(4)]
        R_bg = [[Res("bg") for c in range(3)] for g in range(4)]
        R_cg = [[Res("cg") for c in range(3)] for g in range(4)]
        R_m = [[Res("m") for c in range(3)] for m in range(16)]
        R_w = [Res("w%d" % i) for i in range(14)]
        R_gbc = Res("gbc")
        R_cst = Res("cst")
        R_vec = Res("vec")
        R_AB = Res("AB")
        R_misc = Res("misc")
        R_const = Res("const")
        R_sgn, R_bsg, R_wsT, R_knw, R_wp = Res("sgn"), Res("bsg"), Res("wsT"), Res("knw"), Res("wp")
        R_stk, R_stv, R_stf = Res("stk"), Res("stv"), Res("stf")
        R_x1 = [Res("x1_%d" % i) for i in range(NT)]
        R_x2 = [Res("x2_%d" % i) for i in range(NT)]
        R_xchin = [Res("xi"), Res("xi")]
        R_xchout = [Res("xo"), Res("xo")]

        def hv(kc, t0, n):
            return hT[:, kc * T + t0: kc * T + t0 + n]

        def qv(h, t0, n):
            return qT[:, h * T + t0: h * T + t0 + n]

        def kTs(g, j0, n):
            return regA[:, g * NKEY + j0: g * NKEY + j0 + n]

        def Vs(tile, g):
            return regA[:, 4608 + tile * 256 + g * 128: 4608 + tile * 256 + g * 128 + 128]

        def kTp(g, j0, n):
            return regA[:, 9216 + g * 512 + j0: 9216 + g * 512 + j0 + n]

        def Vp(tile, g):
            return regA[:, 10240 + tile * 256 + g * 128: 10240 + tile * 256 + g * 128 + 128]

        def bgv(g, t0, n):
            return regA[:, g * T + t0: g * T + t0 + n]

        def cgv(g, t0, n):
            return regA[:, 6144 + g * T + t0: 6144 + g * T + t0 + n]

        def mv(m, t0, n):
            return regMb[:, m * T + t0: m * T + t0 + n]

        def tiles_of(t0, n):
            return list(range(t0 // 128, (t0 + n + 127) // 128))

        NHS = 14

        def wloc(hs):
            return (wbuf, hs * 2048) if hs < 6 else (hT, (hs - 6) * 2048)

        def wv(hs, kc_i, ncols, c0, n):
            t_, o_ = wloc(hs)
            base = o_ + kc_i * ncols + c0
            return t_[:, base: base + n]

        wemitted = {}

        def _emit_load(spec, after):
            hs = spec["hs"]
            d, s_ = spec["dst"], spec["src"]
            if spec["kind"] == "main":
                res = [R_w[hs + i] for i in range(spec["nhalf"])]
                return P.op("pool", lambda e, d=d, s_=s_: e.dma_start(out=d, in_=s_), writes=res, after=after, dma="w%d" % hs)
            o = P.op("pool", lambda e, d=d, s_=s_: e.dma_start(out=d, in_=s_), dma="w%d" % hs)
            R_w[hs].w = o
            return o

        def _cover(sp):
            return range(sp["hs"], sp["hs"] + sp.get("nhalf", 1))

        def _load(spec, after):
            k = state["wk"]
            state["wk"] += 1
            P.clock = k + 1
            spec["late"] = bool(after)
            if wplan is None:
                spec["ready"] = max([o.clock for h in _cover(spec) for o in R_w[h].r] + [0])
                rec.append(spec)
                return _emit_load(spec, after)
            if k not in wemitted:
                wemitted[k] = _emit_load(spec, after)
            kk = k + 1
            if kk < len(wplan) and kk not in wemitted and not wplan[kk]["late"] and wplan[kk]["kind"] == "main" \
                    and wplan[kk]["ready"] <= k:
                wemitted[kk] = _emit_load(wplan[kk], ())
            return wemitted[k]

        def load_w(hs, nhalf, src2d, kc, ncols, kc_off=0, total_cols=None, after=()):
            tc_ = ncols if total_cols is None else total_cols
            t_, o_ = wloc(hs)
            base = o_ + kc_off * tc_
            dst = t_[:, base: base + kc * tc_].rearrange("p (k n) -> p k n", k=kc)
            src = src2d.rearrange("(k p) n -> p k n", p=128)
            return _load({"kind": "main", "hs": hs, "nhalf": nhalf, "dst": dst, "src": src}, tuple(after))

        def load_w_extra(hs, src2d, kc, ncols, kc_off):
            t_, o_ = wloc(hs)
            base = o_ + kc_off * ncols
            dst = t_[:, base: base + kc * ncols].rearrange("p (k n) -> p k n", k=kc)
            src = src2d.rearrange("(k p) n -> p k n", p=128)
            return _load({"kind": "extra", "hs": hs, "dst": dst, "src": src}, ())

        def mm_group(out_ap, pairs, reads, writes, after=()):
            def fn(e, pairs=pairs, out_ap=out_ap):
                n = len(pairs)
                ins = None
                for i, (l, r) in enumerate(pairs):
                    ins = e.matmul(out_ap, l, r, start=(i == 0), stop=(i == n - 1))
                return ins
            return P.op("pe", fn, reads, writes, after)

        XT = [regM[:, 0:2048], regM[:, 2048:4096]]
        R_xt = [Res("xt0"), Res("xt1")]
        XN = regM[:, 4096:5120].bitcast(BF16)
        R_xn = Res("xn")
        SQJ = regM[:, 5120:6144].bitcast(BF16)
        R_sqj = Res("sqj")
        TF = [regM[:, 6144 + i * 512: 6144 + (i + 1) * 512] for i in range(6)]
        R_tf = [Res("tf%d" % i) for i in range(6)]
        TB = [regM[:, 9216 + i * 256: 9216 + (i + 1) * 256].bitcast(BF16) for i in range(6)]
        R_tb = [Res("tb%d" % i) for i in range(6)]
        SQT = [SQJ[:, i * 512:(i + 1) * 512] for i in range(4)]
        R_sqt = [Res("sqt%d" % i) for i in range(4)]
        TFX = [regM[:, 2304 + i * 512: 2304 + (i + 1) * 512] for i in range(2)]
        R_tfx = [Res("tfx%d" % i) for i in range(2)]
        QTB = [(TB[0], R_tb[0]), (TB[1], R_tb[1]), (TB[2], R_tb[2]), (TB[3], R_tb[3]),
               (SQT[0], R_sqt[0]), (SQT[1], R_sqt[1]), (SQT[2], R_sqt[2]), (SQT[3], R_sqt[3])]
        QTF = [(TF[0], R_tf[0]), (TF[3], R_tf[3]), (TF[1], R_tf[1]), (TF[4], R_tf[4]),
               (TF[2], R_tf[2]), (TF[5], R_tf[5]), (TFX[0], R_tfx[0]), (TFX[1], R_tfx[1])]
        SCB = regM[:, 6144:6144 + 16].bitcast(BF16)
        SCF = regM[:, 6144 + 64:6144 + 96]
        SREP = [regM[:, 0:1024].bitcast(BF16), regM[:, 1024:2048].bitcast(BF16)]

        o_c = P.op("pool", lambda e: e.dma_start(out=cst[:, 0:256], in_=cmat), writes=[R_cst], dma="c0")
        P.op("dve", lambda e: e.memset(cst[:, 256:384], 1.0 / 128), writes=[R_const])
        P.op("dve", lambda e: e.memset(cst[:, 384:512], 1.0), writes=[R_const])
        P.op("dve", lambda e: e.memset(epsb[:], EPS), writes=[R_const])
        P.op("pool", lambda e: e.dma_start(out=rc[:], in_=ropec), writes=[R_const], dma="c1")
        P.op("pool", lambda e: e.dma_start(out=rs[:], in_=ropes), writes=[R_const], dma="c2")
        P.op("sp", lambda e: e.dma_start(out=mlr_t[:], in_=mlr), writes=[R_const], dma="c3")
        R_cT = Res("cT")
        CTF = regM[:, 11000:11032]
        P.op("sp", lambda e: e.dma_start(out=CTF, in_=cT), writes=[R_cT], dma="c4")

        sems_needed = set()
        state = {"wrot": 0, "last_p7_pe": None, "qkcnt": 0, "wk": 0}

        def next_slot_full():
            s = state["wrot"] % 3
            state["wrot"] += 1
            return s * 2

        x_src = [xs, x1]
        x_dst = [x1, x2]
        R_xsrc = [[Res("xs") for i in range(NT)], R_x1]
        R_xdst = [R_x1, R_x2]
        psrot = {"a": 0, "b": 0}

        def bank_main():
            b = psrot["a"] % 4
            psrot["a"] += 1
            return b

        def bank_aux():
            b = 4 + psrot["b"] % 4
            psrot["b"] += 1
            return b

        mvec = sb("mvec", [128, 128], F32)
        scf = sb("scf", [128, 32], F32)
        scb = sb("scb", [128, 32], BF16)
        st2 = sb("st2", [128, 8], F32)
        R_mvec = Res("mvec")
        R_sc = Res("sc")
        R_st2 = [Res("st2a"), Res("st2b")]
        R_ABl = [Res("AB0"), Res("AB1")]
        R_rep = [Res("rep%d" % i) for i in range(8)]
        XN1 = regM[:, 11040:12064].bitcast(BF16)
        XNs = [XN, XN1]
        R_xns = [R_xn, Res("xn1")]
        for l_ in range(2):
            P.op("sp", lambda e, l_=l_: e.dma_start(out=mvec[:, l_ * 64:(l_ + 1) * 64], in_=vecs[l_][:, 0:64]), writes=[R_mvec], dma="mv%d" % l_)
        P.op("act", lambda e: e.activation(out=scf[:], in_=CTF, func=AF.Silu), reads=[R_cT], writes=[R_sc])
        P.op("dve", lambda e: e.tensor_copy(out=scb[:], in_=scf[:]), reads=[R_sc], writes=[R_sc])
        BM = 3

        def mod_steps(L, part):
            steps = []
            if part == 'ss':
                nblk, colbase, wbase = 16, 0, 0
            else:
                nblk, colbase, wbase = 8, 64, 4096
            first = [True]
            for blk in range(nblk):
                def step(blk=blk):
                    hs = next_slot_full()
                    load_w(hs, 2, w_ada[L][:, wbase + blk * 256: wbase + (blk + 1) * 256], KC, 256)
                    for jj in range(2):
                        j = blk * 2 + jj
                        pairs = [(wv(hs, kc, 256, jj * 128, 128), scb[:, kc * 2: kc * 2 + 2]) for kc in range(KC)]
                        mm_group(ps[BM][:, colbase + j * 2: colbase + j * 2 + 2], pairs, reads=[R_w[hs], R_w[hs + 1], R_sc],
                                 writes=[R_ps[BM]] if first[0] else [])
                        first[0] = False
                steps.append(step)

            def fin():
                lastpe = P.last("pe")
                ao = L * 64
                if part == 'ss':
                    MODT3 = TF[4][:, 0:64].rearrange("p (j s) -> p j s", s=2)
                    PS3 = ps[BM][:, 0:64].rearrange("p (j s) -> p j s", s=2)
                    for s_ in range(2):
                        P.op("dve", lambda e, s_=s_: e.tensor_tensor(out=MODT3[:, :, s_], in0=PS3[:, :, s_], in1=mvec[:, ao + 16: ao + 48], op=ALU.add),
                             reads=[R_ps[BM], R_mvec], writes=[R_tf[4]], after=[lastpe])
                    for s_ in range(2):
                        P.op("dve", lambda e, s_=s_: e.scalar_tensor_tensor(
                            out=AB[:, ao + s_ * 16: ao + (s_ + 1) * 16], in0=MODT3[:, 16:32, s_], scalar=1.0,
                            in1=mvec[:, ao: ao + 16], op0=ALU.add, op1=ALU.mult), reads=[R_tf[4], R_mvec], writes=[R_ABl[L]])
                        P.op("dve", lambda e, s_=s_: e.tensor_copy(out=AB[:, ao + 32 + s_ * 16: ao + 32 + (s_ + 1) * 16], in_=MODT3[:, 0:16, s_]),
                             reads=[R_tf[4]], writes=[R_ABl[L]])
                else:
                    GT = TF[4][:, 64:96]
                    GT3 = GT.rearrange("p (j s) -> p j s", s=2)
                    PS3 = ps[BM][:, 64:96].rearrange("p (j s) -> p j s", s=2)
                    for s_ in range(2):
                        P.op("dve", lambda e, s_=s_: e.tensor_tensor(out=GT3[:, :, s_], in0=PS3[:, :, s_], in1=mvec[:, ao + 48: ao + 64], op=ALU.add),
                             reads=[R_ps[BM], R_mvec], writes=[R_tf[4]], after=[lastpe])
                    REP = TF[3].bitcast(BF16)
                    cnt = 0
                    for s_ in range(2):
                        for q4 in range(4):
                            lp = None
                            for k4 in range(4):
                                kc = q4 * 4 + k4
                                ri = cnt % 8
                                rep = REP[:, ri * 128:(ri + 1) * 128]
                                P.op("dve", lambda e, rep=rep, kc=kc, s_=s_: e.tensor_scalar(
                                    out=rep, in0=ones, scalar1=GT[:, kc * 2 + s_: kc * 2 + s_ + 1], scalar2=None, op0=ALU.mult),
                                    reads=[R_tf[4], R_const], writes=[R_rep[ri]] + ([R_tf[3]] if cnt < 8 else []))
                                lp = P.op("pe", lambda e, rep=rep, k4=k4: e.matmul(ps[BM][:, k4 * 128:(k4 + 1) * 128], rep, ident, start=True, stop=True),
                                          reads=[R_rep[ri], R_cst, R_tf[3]], writes=[R_ps[BM]] if k4 == 0 else [])
                                cnt += 1
                            P.op("act", lambda e, s_=s_, q4=q4: e.activation(out=gbc[:, s_ * D + q4 * 512: s_ * D + (q4 + 1) * 512], in_=ps[BM][:], func=AF.Copy),
                                 reads=[R_ps[BM]], writes=[R_gbc], after=[lp])
            steps.append(fin)
            return steps

        for l in range(nlayers):
            P.op("sp", lambda e, l=l: e.dma_start(out=vec[:], in_=vecs[l]), writes=[R_vec], dma="v0")
            P.op("sp", lambda e, l=l: e.dma_start(out=sgn_t[:], in_=sgn[l]), writes=[R_sgn], dma="v1")
            P.op("sp", lambda e, l=l: e.dma_start(out=bsg_t[:], in_=bsg[l]), writes=[R_bsg], dma="v2")
            P.op("pool", lambda e, l=l: e.dma_start(out=wsT_t[:], in_=wsT[l]), writes=[R_wsT], dma="v3")
            P.op("sp", lambda e, l=l: e.dma_start(out=knw_t[:], in_=knw[l]), writes=[R_knw], dma="v4")
            P.op("pool", lambda e, l=l: e.dma_start(
                out=wp_t[:].rearrange("p (g n) -> p g n", g=4),
                in_=w_pool[l].rearrange("g p n -> p g n")), writes=[R_wp], dma="v5")

            if l == 0:
                pending = mod_steps(0, 'ss')
            else:
                pending = []
            for i in range(NT):
                sl = i % 2
                so = sl * 4
                P.op("sp", lambda e, l=l, i=i, sl=sl: e.dma_start(out=XT[sl], in_=x_src[l][i * 128:(i + 1) * 128, :]),
                     reads=[R_xsrc[l][i]], writes=[R_xt[sl]], dma="xt%d" % sl)
                P.op("act", lambda e, sl=sl, so=so: e.activation(out=SQJ, in_=XT[sl], func=AF.Square, accum_out=st2[:, so:so + 1]),
                     reads=[R_xt[sl]], writes=[R_sqj, R_st2[sl]])
                P.op("act", lambda e, so=so: e.activation(out=st2[:, so + 1:so + 2], in_=st2[:, so:so + 1], func=AF.Sqrt, scale=1.0 / D, bias=epsb[:, 0:1]),
                     reads=[R_const], writes=[R_st2[sl]])
                P.op("dve", lambda e, so=so: e.reciprocal(out=st2[:, so + 2:so + 3], in_=st2[:, so + 1:so + 2]), reads=[], writes=[R_st2[sl]])
                P.op("dve", lambda e, sl=sl, so=so: e.tensor_scalar(out=XNs[sl], in0=XT[sl], scalar1=st2[:, so + 2:so + 3], scalar2=None, op0=ALU.mult),
                     reads=[R_xt[sl], R_st2[sl]], writes=[R_xns[sl]])
                for half in range(2):
                    b = 4 + 2 * sl + half
                    pb = ps[b][:].bitcast(BF16)

                    def tr(e, half=half, pb=pb, sl=sl):
                        ins = None
                        for k8 in range(8):
                            kc = half * 8 + k8
                            ins = e.transpose(pb[:, k8 * 128:(k8 + 1) * 128], XNs[sl][:, kc * 128:(kc + 1) * 128], ident)
                        return ins
                    P.op("pe", tr, reads=[R_xns[sl], R_cst], writes=[R_ps[b]])
                    dst3 = hT[:, :].rearrange("p (k t) -> p k t", k=KC)[:, half * 8:(half + 1) * 8, i * 128:(i + 1) * 128]
                    src3 = pb.rearrange("p (k t) -> p k t", k=8)
                    if half == 0:
                        P.op("act", lambda e, dst3=dst3, src3=src3: e.activation(out=dst3, in_=src3, func=AF.Copy), reads=[R_ps[b]], writes=[R_h[i]],
                             after=[state["last_p7_pe"]])
                    else:
                        P.op("dve", lambda e, dst3=dst3, src3=src3: e.tensor_copy(out=dst3, in_=src3), reads=[R_ps[b]], writes=[R_h[i]],
                             after=[state["last_p7_pe"]])
                for _ in range(2):
                    if pending:
                        pending.pop(0)()
            while pending:
                pending.pop(0)()
            ao = l * 64
            for kc in range(KC):
                P.op("dve", lambda e, kc=kc, ao=ao: e.tensor_scalar(
                    out=hv(kc, 0, NS), in0=hv(kc, 0, NS), scalar1=AB[:, ao + 16 + kc: ao + 17 + kc], scalar2=AB[:, ao + 48 + kc: ao + 49 + kc],
                    op0=ALU.mult, op1=ALU.add), reads=[R_ABl[l]], writes=R_h[0:8])
                P.op("act", lambda e, kc=kc, ao=ao: e.activation(
                    out=hv(kc, NS, 512), in_=hv(kc, NS, 512), func=AF.Identity, scale=AB[:, ao + kc: ao + kc + 1], bias=AB[:, ao + 32 + kc: ao + 33 + kc]),
                    reads=[R_ABl[l]], writes=R_h[8:12])

            if stop == 'norm':
                break
            Rh_chunk = [[R_h[c * 4 + j] for j in range(4)] for c in range(3)]
            XSTG = regM[:, 0:XCH // 2].bitcast(BF16)
            R_xstg = Res("xstg")
            bar0 = P.barrier()

            def qk_process(b, dst_ap, dst_res, is_sample, c, wcol, ncols_tok, extra_after=()):
                n = ncols_tok
                i0 = state["qkcnt"]
                state["qkcnt"] += 1
                tb0, rb0 = QTB[(i0 % 4) * 2]
                tb1, rb1 = QTB[(i0 % 4) * 2 + 1]
                P.op("act", lambda e: e.activation(out=tb0[:, 0:n], in_=ps[b][:, 0:n], func=AF.Square),
                     reads=[R_ps[b]], writes=[rb0], after=extra_after)
                P.op("act", lambda e: e.activation(out=tb1[:, 0:n], in_=ps[b][:, 0:n], func=AF.Copy, scale=vec[:, wcol:wcol + 1]),
                     reads=[R_ps[b], R_vec], writes=[rb1])
                b2 = bank_aux()
                mm_group(ps[b2][:, 0:n], [(onesm, tb0[:, 0:n])], reads=[rb0, R_const], writes=[R_ps[b2]])
                tf0, rf0 = QTF[(i0 % 4) * 2]
                P.op("act", lambda e: e.activation(out=tf0[:, 0:n], in_=ps[b2][:, 0:n], func=AF.Sqrt, bias=epsb[:, 0:1]),
                     reads=[R_ps[b2], R_const], writes=[rf0])
                P.op("dve", lambda e: e.reciprocal(out=tf0[:, 0:n], in_=tf0[:, 0:n]), reads=[rf0], writes=[rf0])
                if is_sample:
                    b3 = bank_aux()
                    mm_group(ps[b3][:, 0:n], [(perm, tb1[:, 0:n])], reads=[rb1, R_cst], writes=[R_ps[b3]])
                    tf1, rf1 = QTF[(i0 % 4) * 2 + 1]
                    t0 = c * 512
                    P.op("pool", lambda e: e.tensor_tensor(out=tf1[:, 0:n], in0=tb1[:, 0:n], in1=rc[:, t0:t0 + n], op=ALU.mult),
                         reads=[rb1, R_const], writes=[rf1])
                    P.op("dve", lambda e: e.tensor_tensor(out=tb0[:, 0:n], in0=ps[b3][:, 0:n], in1=rs[:, t0:t0 + n], op=ALU.mult),
                         reads=[R_ps[b3], R_const], writes=[rb0])
                    P.op("pool", lambda e: e.tensor_tensor(out=tf1[:, 0:n], in0=tf1[:, 0:n], in1=tb0[:, 0:n], op=ALU.add),
                         reads=[rb0, rf1], writes=[rf1])
                    P.op("dve", lambda e: e.tensor_tensor(out=dst_ap, in0=tf1[:, 0:n], in1=tf0[:, 0:n], op=ALU.mult),
                         reads=[rf1, rf0], writes=[dst_res])
                else:
                    P.op("dve", lambda e: e.tensor_tensor(out=dst_ap, in0=tb1[:, 0:n], in1=tf0[:, 0:n], op=ALU.mult),
                         reads=[rb1, rf0], writes=[dst_res])

            hs = next_slot_full()
            load_w(hs, 2, w_in[l][:, 1024:1280], KC, 256)
            hs_k = hs
            for g in range(2):
                for c in range(3):
                    b = bank_main()
                    pairs = [(wv(hs, kc, 256, g * 128, 128), hv(kc, c * 512, 512)) for kc in range(KC)]
                    mm_group(ps[b][:], pairs, reads=[R_w[hs], R_w[hs + 1]] + Rh_chunk[c], writes=[R_ps[b]])
                    if c < 2:
                        dst = XSTG[:, g * 1024 + c * 512: g * 1024 + (c + 1) * 512]
                        qk_process(b, dst, R_xstg, True, c, 117, 512, extra_after=bar0)
                    else:
                        qk_process(b, kTp(g, 0, 512), R_kTp[g], False, c, 117, 512, extra_after=bar0 + [None])
            hs_v = next_slot_full()
            load_w(hs_v, 2, w_in[l][:, 1280:1536], KC, 256)
            OT = TF
            for i in range(NT):
                b = bank_main()
                pairs = [(hv(kc, i * 128, 128), wv(hs_v, kc, 256, 0, 256)) for kc in range(KC)]
                mm_group(ps[b][:, 0:256], pairs, reads=[R_w[hs_v], R_w[hs_v + 1], R_h[i]], writes=[R_ps[b]])
                if i < 8:
                    P.op("act", lambda e, b=b, i=i: e.activation(out=XSTG[:, 2048 + i * 256: 2048 + (i + 1) * 256], in_=ps[b][:, 0:256],
                                                               func=AF.Copy), reads=[R_ps[b]], writes=[R_xstg])
                else:
                    ip = i - 8
                    P.op("act", lambda e, b=b, ip=ip: e.activation(out=regA[:, 10240 + ip * 256: 10240 + (ip + 1) * 256], in_=ps[b][:, 0:256],
                                                                 func=AF.Copy), reads=[R_ps[b]], writes=[R_Vp[ip]])
                    tfi = ip % 2
                    P.op("dve", lambda e, b=b, tfi=tfi: e.tensor_copy(out=TF[tfi][:, 0:256], in_=ps[b][:, 0:256]),
                         reads=[R_ps[b]], writes=[R_tf[tfi]])
                    P.op("sp", lambda e, l=l, ip=ip, tfi=tfi: e.dma_start(out=sv[l][ip * 128:(ip + 1) * 128, :], in_=TF[tfi][:, 0:256]),
                         reads=[R_tf[tfi]], dma="osv%d" % tfi)
                    b2 = bank_main()
                    pairs = [(hv(kc, i * 128, 128), wv(hs_k, kc, 256, 0, 256)) for kc in range(KC)]
                    mm_group(ps[b2][:, 0:256], pairs, reads=[R_w[hs_k], R_w[hs_k + 1], R_h[i]], writes=[R_ps[b2]])
                    tk = 2 + ip % 2
                    for g in range(2):
                        P.op("act", lambda e, b2=b2, g=g: e.activation(out=TB[4][:, 0:128], in_=ps[b2][:, g * 128:(g + 1) * 128], func=AF.Square,
                                                                     accum_out=st[:, 4 + g: 5 + g]), reads=[R_ps[b2]], writes=[R_tb[4], R_stk])
                    P.op("act", lambda e: e.activation(out=st[:, 6:8], in_=st[:, 4:6], func=AF.Sqrt, scale=1.0 / 128, bias=epsb[:, 0:1]),
                         reads=[R_stk, R_const], writes=[R_stk])
                    P.op("dve", lambda e: e.reciprocal(out=st[:, 8:10], in_=st[:, 6:8]), reads=[R_stk], writes=[R_stk])
                    for g in range(2):
                        P.op("dve", lambda e, b2=b2, g=g, tk=tk: e.scalar_tensor_tensor(
                            out=TF[tk][:, g * 128:(g + 1) * 128], in0=ps[b2][:, g * 128:(g + 1) * 128], scalar=st[:, 8 + g: 9 + g],
                            in1=knw_t[:], op0=ALU.mult, op1=ALU.mult), reads=[R_ps[b2], R_stk, R_knw], writes=[R_tf[tk]])
                    P.op("sp", lambda e, l=l, ip=ip, tk=tk: e.dma_start(out=sk[l][ip * 128:(ip + 1) * 128, :], in_=TF[tk][:, 0:256]),
                         reads=[R_tf[tk]], dma="osk%d" % (tk - 2))
            for kc in range(KC):
                pass
            P.op("pool", lambda e: e.tensor_copy(
                out=XSTG[:, 4096:4352].rearrange("p (k t) -> p k t", k=KC)[:, :, 0:8],
                in_=hT[:, :].rearrange("p (k t) -> p k t", k=KC)[:, :, 0:8]), reads=[R_h[0]], writes=[R_xstg], after=bar0)
            P.op("pool", lambda e: e.tensor_copy(
                out=XSTG[:, 4096:4352].rearrange("p (k t) -> p k t", k=KC)[:, :, 8:16],
                in_=hT[:, :].rearrange("p (k t) -> p k t", k=KC)[:, :, NS - 8:NS]), reads=[R_h[7]], writes=[R_xstg])
            P.op("pool", lambda e, l=l: e.dma_start(out=xch_in[l][:, :], in_=XSTG), reads=[R_xstg], writes=[R_xchin[l]], dma="xi")
            P.op("pool", lambda e, l=l: e.collective_compute(
                "AllGather", ALU.bypass, replica_groups=groups,
                ins=[xch_in[l][:, :]], outs=[xch_out[l][:, :]]), reads=[R_xchin[l]], writes=[R_xchout[l]], dma="cc", inc=1)
            HH = TB[5][:, 0:256]
            n_sp_before = len(P.ops["sp"])
            R_hh = R_tb[5]
            for r in range(2):
                P.op("sp", lambda e, l=l, r=r: e.dma_start(
                    out=regA[:, 0:4608].rearrange("p (g j) -> p g j", g=2)[:, :, r * 1024:(r + 1) * 1024],
                    in_=xch_out[l][r * 128:(r + 1) * 128, 0:2048].rearrange("p (g j) -> p g j", g=2)),
                    reads=[R_xchout[l]], writes=[R_kTs] if r == 0 else [], after=bar0, dma="gk%d" % r)
                P.op("sp", lambda e, l=l, r=r: e.dma_start(
                    out=regA[:, 4608 + r * 2048: 4608 + (r + 1) * 2048], in_=xch_out[l][r * 128:(r + 1) * 128, 2048:4096]),
                    reads=[R_xchout[l]], writes=[R_Vs] if r == 0 else [], after=bar0, dma="gv%d" % r)
            P.op("sp", lambda e, l=l: e.dma_start(
                out=HH.rearrange("p (k t) -> p k t", k=KC)[:, :, 0:8],
                in_=xch_out[l][0:128, 4096:4352].rearrange("p (k t) -> p k t", k=KC)[:, :, 8:16]),
                reads=[R_xchout[l]], writes=[R_hh], dma="gh0")
            o_hh = P.op("sp", lambda e, l=l: e.dma_start(
                out=HH.rearrange("p (k t) -> p k t", k=KC)[:, :, 8:16],
                in_=xch_out[l][128:256, 4096:4352].rearrange("p (k t) -> p k t", k=KC)[:, :, 0:8]),
                reads=[R_xchout[l]], dma="gh1")
            gath_ops = [o for o in P.ops["sp"][n_sp_before:] if o.isdma]
            o_cv = P.op("pool", lambda e, l=l: e.dma_start(
                out=regA[:, 4608 + 16 * 256: 4608 + 18 * 256].rearrange("p (t n) -> p t n", t=2),
                in_=cv[l].rearrange("(t p) n -> p t n", p=128)), after=bar0, dma="cv")
            CKT = TB[4]
            P.op("pool", lambda e, l=l: e.dma_start(out=CKT.rearrange("p (t n) -> p t n", t=2),
                                                   in_=ck[l].rearrange("(t p) n -> p t n", p=128)), writes=[R_tb[4]], dma="ck")
            bck = 7
            pbk = ps[bck][:].bitcast(BF16)

            def trk(e):
                ins = None
                for t_ in range(2):
                    for g in range(2):
                        ins = e.transpose(pbk[:, (t_ * 2 + g) * 128:(t_ * 2 + g + 1) * 128],
                                          CKT[:, t_ * 256 + g * 128: t_ * 256 + (g + 1) * 128], ident)
                return ins
            P.op("pe", trk, reads=[R_tb[4], R_cst], writes=[R_ps[bck]])
            o_ckw = None
            for t_ in range(2):
                for g in range(2):
                    o_ckw = P.op("dve", lambda e, t_=t_, g=g: e.tensor_copy(
                        out=kTs(g, 2048 + t_ * 128, 128), in_=pbk[:, (t_ * 2 + g) * 128:(t_ * 2 + g + 1) * 128]),
                        reads=[R_ps[bck]], after=bar0)
            for qb in range(4):
                hs = next_slot_full()
                load_w(hs, 2, w_in[l][:, qb * 256:(qb + 1) * 256], KC, 256)
                for hh in range(2):
                    h = qb * 2 + hh
                    for c in range(3):
                        b = bank_main()
                        pairs = [(wv(hs, kc, 256, hh * 128, 128), hv(kc, c * 512, 512)) for kc in range(KC)]
                        mm_group(ps[b][:], pairs, reads=[R_w[hs], R_w[hs + 1]] + Rh_chunk[c], writes=[R_ps[b]])
                        qk_process(b, qv(h, c * 512, 512), R_q[h][c], c < 2, c, 116, 512)

            if stop == 'qkv':
                break
            def attention(h, qt0, nq, keys, resq, extra_reads, extra_after):
                g = h // 4
                par = h % 2
                bo, bd = (4, 5) if par == 0 else (6, 7)
                nk = len(keys)
                sbanks = [0, 1, 2]
                e_ops = []

                def s_mm(kt):
                    b = sbanks[kt % 3]
                    return mm_group(ps[b][:, 0:nq], [(keys[kt][0], qv(h, qt0, nq))], reads=[resq] + extra_reads,
                                    writes=[R_ps[b]], after=extra_after)

                def e_act(kt):
                    b = sbanks[kt % 3]
                    tb = TB[kt % 4]
                    return P.op("act", lambda e, b=b, tb=tb: e.activation(out=tb[:, 0:nq], in_=ps[b][:, 0:nq], func=AF.Exp, scale=ATT_SCALE),
                                reads=[R_ps[b]], writes=[R_tb[kt % 4]])

                def pv_mm(kt):
                    tb = TB[kt % 4]

                    def fn(e, kt=kt, tb=tb):
                        e.matmul(ps[bo][:, 0:nq], keys[kt][1], tb[:, 0:nq], start=(kt == 0), stop=(kt == nk - 1))
                        return e.matmul(ps[bd][:, 0:nq], ones, tb[:, 0:nq], start=(kt == 0), stop=(kt == nk - 1))
                    wr = [R_ps[bo], R_ps[bd]] if kt == 0 else []
                    return P.op("pe", fn, reads=[R_tb[kt % 4], R_const] + extra_reads, writes=wr)

                s_mm(0)
                e_act(0)
                if nk > 1:
                    s_mm(1)
                    e_act(1)
                lastpv = None
                for kt in range(nk):
                    lastpv = pv_mm(kt)
                    if kt + 2 < nk:
                        s_mm(kt + 2)
                        e_act(kt + 2)
                tfr = TF[par]
                P.op("dve", lambda e: e.reciprocal(out=tfr[:, 0:nq], in_=ps[bd][:, 0:nq]), reads=[R_ps[bd]], writes=[R_tf[par]], after=[lastpv])
                P.op("dve", lambda e: e.tensor_tensor(out=qv(h, qt0, nq), in0=ps[bo][:, 0:nq], in1=tfr[:, 0:nq], op=ALU.mult),
                     reads=[R_ps[bo], R_tf[par]], writes=[resq], after=[lastpv])

            pending = mod_steps(l, 'gate') + (mod_steps(l + 1, 'ss') if l + 1 < nlayers else [])
            for s_ in range(2):
                for h in range(8):
                    g = h // 4
                    keys = [(kTp(g, s_ * 256 + t_ * 128, 128), Vp(s_ * 2 + t_, g)) for t_ in range(2)]
                    attention(h, 1024 + s_ * 256, 256, keys, R_q[h][2], [R_kTp[g], R_Vp[s_ * 2], R_Vp[s_ * 2 + 1]], [])
            gdeps = [P.ops["sp"][-1]]
            exch_after = gath_ops + [o_cv, o_ckw]
            for c in range(2):
                for h in range(8):
                    g = h // 4
                    keys = [(kTs(g, kt * 128, 128), Vs(kt, g)) for kt in range(18)]
                    attention(h, c * 512, 512, keys, R_q[h][c], [R_kTs, R_Vs], exch_after)
                    for _ in range(2):
                        if pending:
                            pending.pop(0)()
            while pending:
                pending.pop(0)()
            last_att_pe = P.last("pe")

            if stop == 'att':
                break
            for gb_ in range(4):
                hs = next_slot_full()
                load_w(hs, 2, w_in[l][:, 1536 + gb_ * 256: 1536 + (gb_ + 1) * 256], KC, 256)
                for hh in range(2):
                    h = gb_ * 2 + hh
                    for c in range(3):
                        b = bank_main()
                        pairs = [(wv(hs, kc, 256, hh * 128, 128), hv(kc, c * 512, 512)) for kc in range(KC)]
                        mm_group(ps[b][:], pairs, reads=[R_w[hs], R_w[hs + 1]] + Rh_chunk[c], writes=[R_ps[b]])
                        tb = TB[(h * 3 + c) % 4]
                        rtb = R_tb[(h * 3 + c) % 4]
                        P.op("act", lambda e, b=b, tb=tb: e.activation(out=tb[:], in_=ps[b][:], func=AF.Silu), reads=[R_ps[b]], writes=[rtb])
                        P.op("dve", lambda e, tb=tb, h=h, c=c: e.tensor_tensor(out=qv(h, c * 512, 512), in0=qv(h, c * 512, 512), in1=tb[:], op=ALU.mult),
                             reads=[rtb], writes=[R_q[h][c]])

            if stop == 'ga':
                break
            VC = regM[:, 0:3072].bitcast(BF16)
            R_vc = [Res("vc%d" % i) for i in range(NT)]
            hs_vb = []
            for half in range(2):
                hs = next_slot_full()
                load_w(hs, 2, w_in[l][:, 3072 + half * 256: 3072 + (half + 1) * 256], KC, 256)
                hs_vb.append(hs)
            for i in range(NT):
                b = bank_main()
                for half in range(2):
                    pairs = [(hv(kc, i * 128, 128), wv(hs_vb[half], kc, 256, 0, 256)) for kc in range(KC)]
                    mm_group(ps[b][:, half * 256:(half + 1) * 256], pairs, reads=[R_w[hs_vb[half]], R_w[hs_vb[half] + 1], R_h[i]],
                             writes=[R_ps[b]] if half == 0 else [])
                lpe = P.last("pe")
                P.op("act", lambda e, b=b: e.activation(out=SQJ[:, 0:512], in_=ps[b][:], func=AF.Square, accum_out=st[:, 10:11]),
                     reads=[R_ps[b]], writes=[R_sqj, R_stv], after=[lpe, last_att_pe])
                P.op("act", lambda e: e.activation(out=st[:, 11:12], in_=st[:, 10:11], func=AF.Sqrt, scale=1.0 / 512, bias=epsb[:, 0:1]),
                     reads=[R_stv, R_const], writes=[R_stv])
                P.op("dve", lambda e: e.reciprocal(out=st[:, 12:13], in_=st[:, 11:12]), reads=[R_stv], writes=[R_stv])
                P.op("dve", lambda e, b=b, i=i: e.scalar_tensor_tensor(
                    out=VC[:, i * 512:(i + 1) * 512], in0=ps[b][:], scalar=st[:, 12:13], in1=sgn_t[:], op0=ALU.mult, op1=ALU.mult),
                    reads=[R_ps[b], R_stv, R_sgn], writes=[R_vc[i]], after=[lpe, last_att_pe])
            for gp in range(2):
                hs_u = next_slot_full()
                load_w(hs_u, 2, w_in[l][:, 2560 + gp * 256: 2560 + (gp + 1) * 256], KC, 256)
                hs_g = next_slot_full()
                load_w(hs_g, 2, w_in[l][:, 3584 + gp * 256: 3584 + (gp + 1) * 256], KC, 256)
                for gg in range(2):
                    g = gp * 2 + gg
                    for c in range(3):
                        bu = bank_main()
                        mm_group(ps[bu][:], [(wv(hs_u, kc, 256, gg * 128, 128), hv(kc, c * 512, 512)) for kc in range(KC)],
                                 reads=[R_w[hs_u], R_w[hs_u + 1]] + Rh_chunk[c], writes=[R_ps[bu]])
                        bg_ = bank_main()
                        mm_group(ps[bg_][:], [(wv(hs_g, kc, 256, gg * 128, 128), hv(kc, c * 512, 512)) for kc in range(KC)],
                                 reads=[R_w[hs_g], R_w[hs_g + 1]] + Rh_chunk[c], writes=[R_ps[bg_]])
                        bm = bank_aux()

                        def mix(e, g=g, c=c, bm=bm):
                            ins = None
                            for j in range(4):
                                i = c * 4 + j
                                ins = e.matmul(ps[bm][:, j * 128:(j + 1) * 128], VC[:, i * 512 + g * 128: i * 512 + (g + 1) * 128],
                                               wsT_t[:, g * 128:(g + 1) * 128], start=True, stop=True)
                            return ins
                        P.op("pe", mix, reads=[R_vc[c * 4 + j] for j in range(4)] + [R_wsT], writes=[R_ps[bm]])
                        k_ = (g * 3 + c) % 2
                        tfa, tfb = TF[3 + k_ * 1], TF[5 - k_ * 0] if False else TF[4 + k_] if False else TF[3 + k_]
                        tfa = TF[3 + k_]
                        rfa = R_tf[3 + k_]
                        tbx = TB[k_]
                        rbx = R_tb[k_]
                        P.op("act", lambda e, bg_=bg_, tbx=tbx: e.activation(out=tbx[:], in_=ps[bg_][:], func=AF.Silu), reads=[R_ps[bg_]], writes=[rbx])
                        P.op("dve", lambda e, bu=bu, tbx=tbx, tfa=tfa: e.tensor_tensor(out=tfa[:], in0=ps[bu][:], in1=tbx[:], op=ALU.mult),
                             reads=[R_ps[bu], rbx], writes=[rfa])
                        tfc = TF[5]
                        for j in range(4):
                            P.op("dve", lambda e, bm=bm, g=g, tfc=tfc, j=j: e.tensor_tensor(
                                out=tfc[:, j * 128:(j + 1) * 128], in0=ps[bm][:, j * 128:(j + 1) * 128],
                                in1=bsg_t[:, g * 128:(g + 1) * 128], op=ALU.add),
                                reads=[R_ps[bm], R_bsg], writes=[R_tf[5]])
                        P.op("dve", lambda e, g=g, c=c, tfa=tfa, tfc=tfc: e.tensor_tensor(out=bgv(g, c * 512, 512), in0=tfa[:], in1=tfc[:], op=ALU.mult),
                             reads=[rfa, R_tf[5]], writes=[R_bg[g][c]], after=[last_att_pe])

            if stop == 'B':
                break
            ZW = PADL + 1024 + 8 + PADL + 256 + 8 + PADL + 256 + 8
            ZA = regM[:, 0:ZW]
            ZB = regM[:, ZW: 2 * ZW]
            last_5a_pe = P.last("pe")
            R_za, R_zb = Res("za"), Res("zb")
            segs = [(0, 1024, 0), (1024, 256, PADL + 1024 + 8), (1280, 256, 2 * PADL + 1024 + 8 + 256 + 8)]
            wins = [2, 4, 8, 16]
            IV = TF[2]
            for gp in range(2):
                hs_z = next_slot_full()
                load_w(hs_z, 2, w_in[l][:, 4096 + gp * 256: 4096 + (gp + 1) * 256], KC, 256)
                hs_c = next_slot_full()
                load_w(hs_c, 2, w_in[l][:, 4608 + gp * 256: 4608 + (gp + 1) * 256], KC, 256)
                for gg in range(2):
                    g = gp * 2 + gg
                    w_ = wins[g]
                    P.op("pool", lambda e: e.memset(ZA, 0.0), writes=[R_za], after=[last_5a_pe])
                    bh = bank_aux()
                    mm_group(ps[bh][:, 0:16], [(wv(hs_z, kc, 256, gg * 128, 128), HH[:, kc * 16:(kc + 1) * 16]) for kc in range(KC)],
                             reads=[R_w[hs_z], R_w[hs_z + 1], R_hh], writes=[R_ps[bh]], after=[o_hh])
                    P.op("dve", lambda e, bh=bh: e.tensor_tensor(out=ZA[:, PADL - 8:PADL], in0=ps[bh][:, 0:8], in1=mlr_t[:, 0:8], op=ALU.mult),
                         reads=[R_ps[bh], R_const], writes=[R_za])
                    P.op("dve", lambda e, bh=bh: e.tensor_tensor(out=ZA[:, PADL + 1024:PADL + 1032], in0=ps[bh][:, 8:16], in1=mlr_t[:, 8:16], op=ALU.mult),
                         reads=[R_ps[bh], R_const], writes=[R_za])
                    for c in range(3):
                        bz = bank_main()
                        mm_group(ps[bz][:], [(wv(hs_z, kc, 256, gg * 128, 128), hv(kc, c * 512, 512)) for kc in range(KC)],
                                 reads=[R_w[hs_z], R_w[hs_z + 1]] + Rh_chunk[c], writes=[R_ps[bz]])
                        if c < 2:
                            P.op("act", lambda e, bz=bz, c=c: e.activation(out=ZA[:, PADL + c * 512: PADL + (c + 1) * 512], in_=ps[bz][:], func=AF.Copy),
                                 reads=[R_ps[bz]], writes=[R_za])
                        else:
                            for s2 in range(2):
                                eo = segs[1 + s2][2] + PADL
                                P.op("act", lambda e, bz=bz, s2=s2, eo=eo: e.activation(out=ZA[:, eo:eo + 256], in_=ps[bz][:, s2 * 256:(s2 + 1) * 256], func=AF.Copy),
                                     reads=[R_ps[bz]], writes=[R_za])
                    TBS = [TB[0], TB[1], TB[3]]
                    R_tbs = [R_tb[0], R_tb[1], R_tb[3]]
                    for c in range(3):
                        bc = bank_main()
                        mm_group(ps[bc][:], [(wv(hs_c, kc, 256, gg * 128, 128), hv(kc, c * 512, 512)) for kc in range(KC)],
                                 reads=[R_w[hs_c], R_w[hs_c + 1]] + Rh_chunk[c], writes=[R_ps[bc]])
                        P.op("act", lambda e, bc=bc, c=c: e.activation(out=TBS[c][:], in_=ps[bc][:], func=AF.Silu), reads=[R_ps[bc]], writes=[R_tbs[c]])
                    cur, curR, oth, othR = ZA, R_za, ZB, R_zb
                    zsrc = ZA
                    steps = []
                    s_ = 1
                    while s_ < w_:
                        steps.append(s_)
                        s_ *= 2
                    ZC = regM[:, 2 * ZW: 3 * ZW]
                    R_zc = Res("zc")
                    bufs = [(ZB, R_zb), (ZC, R_zc)]
                    src, srcR = ZA, R_za
                    for si, s_ in enumerate(steps):
                        dst, dstR = bufs[si % 2]
                        P.op("pool", lambda e, dst=dst, src=src, s_=s_: e.tensor_tensor(out=dst[:, s_:ZW], in0=src[:, s_:ZW], in1=src[:, 0:ZW - s_], op=ALU.add),
                             reads=[srcR], writes=[dstR])
                        P.op("pool", lambda e, dst=dst, src=src, s_=s_: e.tensor_copy(out=dst[:, 0:s_], in_=src[:, 0:s_]), reads=[srcR], writes=[dstR])
                        src, srcR = dst, dstR
                    bsh = w_ // 2 - 1
                    for c in range(3):
                        P.op("sp", lambda e, g=g, c=c: e.dma_start(out=IV[:], in_=invc[g:g + 1, c * 512:(c + 1) * 512].partition_broadcast(128)),
                             writes=[R_tf[2]], dma="iv")
                        tfd = TF[0]
                        tbd = TB[2]
                        if c < 2:
                            eo = PADL + c * 512
                            P.op("dve", lambda e, src=src, eo=eo, tfd=tfd, bsh=bsh: e.tensor_tensor(out=tfd[:], in0=src[:, eo + bsh: eo + bsh + 512], in1=IV[:], op=ALU.mult),
                                 reads=[srcR, R_tf[2]], writes=[R_tf[0]])
                            P.op("dve", lambda e, eo=eo, tfd=tfd, tbd=tbd: e.tensor_tensor(out=tbd[:], in0=tfd[:], in1=ZA[:, eo:eo + 512], op=ALU.subtract),
                                 reads=[R_tf[0], R_za], writes=[R_tb[2]])
                        else:
                            for s2 in range(2):
                                eo = segs[1 + s2][2] + PADL
                                P.op("dve", lambda e, src=src, eo=eo, s2=s2, tfd=tfd, bsh=bsh: e.tensor_tensor(
                                    out=tfd[:, s2 * 256:(s2 + 1) * 256], in0=src[:, eo + bsh: eo + bsh + 256], in1=IV[:, s2 * 256:(s2 + 1) * 256], op=ALU.mult),
                                    reads=[srcR, R_tf[2]], writes=[R_tf[0]])
                                P.op("dve", lambda e, eo=eo, s2=s2, tfd=tfd, tbd=tbd: e.tensor_tensor(
                                    out=tbd[:, s2 * 256:(s2 + 1) * 256], in0=tfd[:, s2 * 256:(s2 + 1) * 256], in1=ZA[:, eo:eo + 256], op=ALU.subtract),
                                    reads=[R_tf[0], R_za], writes=[R_tb[2]])
                        bp = bank_aux()
                        mm_group(ps[bp][:], [(wp_t[:, g * 128:(g + 1) * 128], tbd[:])], reads=[R_tb[2], R_wp], writes=[R_ps[bp]])
                        tbs = TBS[c]
                        rtbs = R_tbs[c]
                        P.op("dve", lambda e, bp=bp, g=g, c=c, tbs=tbs: e.scalar_tensor_tensor(
                            out=cgv(g, c * 512, 512), in0=ps[bp][:], scalar=vec[:, 112 + g: 113 + g], in1=tbs[:], op0=ALU.mult, op1=ALU.mult),
                            reads=[R_ps[bp], rtbs, R_vec], writes=[R_cg[g][c]], after=[last_att_pe])

            if stop == 'C':
                break
            bar6 = P.barrier()
            ACC = [t6[:, i * 512:(i + 1) * 512] for i in range(3)]
            R_acc = [Res("acc%d" % i) for i in range(3)]
            TMP = [t6[:, (3 + i) * 512:(4 + i) * 512] for i in range(2)]
            R_tmp = [Res("tmp%d" % i) for i in range(2)]
            SG = [t6b[:, i * 512:(i + 1) * 512] for i in range(2)]
            R_sg = [Res("sg%d" % i) for i in range(2)]
            cnt6 = 0
            brsrc = [w_br_a, w_br_b, w_br_c]
            for m in range(16):
                hsb = 4 + m % 2
                load_w(hsb, 1, w_br_a[l][:, m * 128:(m + 1) * 128], 8, 128, kc_off=0)
                load_w_extra(hsb, w_br_b[l][:, m * 128:(m + 1) * 128], 4, 128, kc_off=8)
                o_wabc = load_w_extra(hsb, w_br_c[l][:, m * 128:(m + 1) * 128], 4, 128, kc_off=12)
                for br in range(3):
                    hsg = (m * 3 + br) % 4
                    load_w(hsg, 1, w_merge[l][:, br * D + m * 128: br * D + (m + 1) * 128], KC, 128)
                    for c in range(3):
                        bgm = bank_main()
                        mm_group(ps[bgm][:], [(wv(hsg, kc, 128, 0, 128), hv(kc, c * 512, 512)) for kc in range(KC)],
                                 reads=[R_w[hsg]] + Rh_chunk[c], writes=[R_ps[bgm]])
                        bbr = bank_main()
                        if br == 0:
                            pairs = [(wv(hsb, kc, 128, 0, 128), qv(kc, c * 512, 512)) for kc in range(8)]
                            rr = [R_q[kc][c] for kc in range(8)]
                        elif br == 1:
                            pairs = [(wv(hsb, 8 + kc, 128, 0, 128), bgv(kc, c * 512, 512)) for kc in range(4)]
                            rr = [R_bg[kc][c] for kc in range(4)]
                        else:
                            pairs = [(wv(hsb, 12 + kc, 128, 0, 128), cgv(kc, c * 512, 512)) for kc in range(4)]
                            rr = [R_cg[kc][c] for kc in range(4)]
                        mm_group(ps[bbr][:], pairs, reads=[R_w[hsb]] + rr, writes=[R_ps[bbr]], after=[o_wabc])
                        k_ = cnt6 % 2
                        cnt6 += 1
                        P.op("act", lambda e, bgm=bgm, k_=k_, br=br, m=m: e.activation(
                            out=SG[k_][:], in_=ps[bgm][:], func=AF.Sigmoid, bias=vec[:, 64 + br * 16 + m: 65 + br * 16 + m]),
                            reads=[R_ps[bgm], R_vec], writes=[R_sg[k_]])
                        if br == 0:
                            P.op("dve", lambda e, bbr=bbr, k_=k_, c=c: e.tensor_tensor(out=ACC[c][:], in0=ps[bbr][:], in1=SG[k_][:], op=ALU.mult),
                                 reads=[R_ps[bbr], R_sg[k_]], writes=[R_acc[c]])
                        else:
                            P.op("dve", lambda e, bbr=bbr, k_=k_: e.tensor_tensor(out=TMP[k_][:], in0=ps[bbr][:], in1=SG[k_][:], op=ALU.mult),
                                 reads=[R_ps[bbr], R_sg[k_]], writes=[R_tmp[k_]])
                            if br == 1:
                                P.op("dve", lambda e, k_=k_, c=c: e.tensor_tensor(out=ACC[c][:], in0=ACC[c][:], in1=TMP[k_][:], op=ALU.add),
                                     reads=[R_tmp[k_]], writes=[R_acc[c]])
                            else:
                                P.op("dve", lambda e, k_=k_, c=c, m=m: e.tensor_tensor(out=mv(m, c * 512, 512), in0=ACC[c][:], in1=TMP[k_][:], op=ALU.add),
                                     reads=[R_tmp[k_], R_acc[c]], writes=[R_m[m][c]], after=bar6)

            if stop == 'merge':
                break
            NX7 = 3
            XC = [t6[:, i * 512:(i + 1) * 512] for i in range(NX7)]
            R_xc = [Res("xc%d" % i) for i in range(NX7)]
            XO = [t6[:, (NX7 + i) * 512:(NX7 + i + 1) * 512] for i in range(NX7)]
            R_xo = [Res("xo%d" % i) for i in range(NX7)]
            bar7 = P.barrier()
            cnt7 = 0
            last_p6_pe = P.last("pe")
            p7slots = [next_slot_full(), next_slot_full(), next_slot_full(), 6, 8, 10, 12]
            p7rot = 0
            for nb in range(4):
                hsp = []
                for kh in range(2):
                    hs = p7slots[p7rot % 7]
                    p7rot += 1
                    load_w(hs, 2, w_out[l][kh * 1024:(kh + 1) * 1024, nb * 512:(nb + 1) * 512], 8, 512,
                           after=[last_p6_pe] if hs >= 6 else [])
                    hsp.append(hs)
                for i in range(NT):
                    tg = i // 4
                    s_ = 1 if i < 8 else 0
                    k_ = cnt7 % NX7
                    cnt7 += 1
                    P.op("act", lambda e, l=l, i=i, nb=nb, k_=k_: e.dma_start(out=XC[k_], in_=x_src[l][i * 128:(i + 1) * 128, nb * 512:(nb + 1) * 512]),
                         reads=[R_xsrc[l][i]], writes=[R_xc[k_]], dma="xc%d" % k_, after=bar7 if cnt7 <= NX7 else [])
                    b = bank_main()
                    pairs = [(mv(kh * 8 + m, i * 128, 128), wv(hsp[kh], m, 512, 0, 512)) for kh in range(2) for m in range(8)]
                    mm_group(ps[b][:], pairs, reads=[R_w[hsp[0]], R_w[hsp[0] + 1], R_w[hsp[1]], R_w[hsp[1] + 1]] + [R_m[m][tg] for m in range(16)],
                             writes=[R_ps[b]])
                    P.op("dve", lambda e, b=b, k_=k_, s_=s_, nb=nb: e.tensor_tensor(
                        out=XO[k_], in0=ps[b][:], in1=gbc[:, s_ * D + nb * 512: s_ * D + (nb + 1) * 512], op=ALU.mult),
                        reads=[R_ps[b], R_gbc], writes=[R_xo[k_]], after=(bar7 if cnt7 <= NX7 else []))
                    P.op("dve", lambda e, k_=k_: e.tensor_tensor(out=XO[k_], in0=XO[k_], in1=XC[k_], op=ALU.add),
                         reads=[R_xc[k_]], writes=[R_xo[k_]])
                    P.op("sp", lambda e, l=l, i=i, nb=nb, k_=k_: e.dma_start(out=x_dst[l][i * 128:(i + 1) * 128, nb * 512:(nb + 1) * 512], in_=XO[k_]),
                         reads=[R_xo[k_]], writes=[R_xdst[l][i]] if nb == 3 else [], dma="xo%d" % k_)
            state["last_p7_pe"] = P.last("pe")

        barF = P.barrier()
        if stop is not None:
            nlayers_f = 0
        else:
            nlayers_f = 1
        FN = regM[:, 4096:6144]
        if nlayers_f:
            P.op("sp", lambda e: e.dma_start(out=FN, in_=fnw), writes=[R_sqj, R_xn], after=barF, dma="fn")
        o_fn = P.ops["sp"][-1]
        last_xo = [o for o in P.ops["sp"] if o.isdma and o.semkey.startswith("xo")][-7:]
        YT = [regM[:, 6144:8192], regM[:, 8192:10240]]
        R_yt = [Res("yt0"), Res("yt1")]
        for i in range(NT if nlayers_f else 0):
            sl = i % 2
            P.op("sp", lambda e, i=i, sl=sl: e.dma_start(out=XT[sl], in_=x_dst[nlayers - 1][i * 128:(i + 1) * 128, :]),
                 reads=[R_xdst[nlayers - 1][i]], writes=[R_xt[sl]], after=barF + last_xo, dma="xt%d" % sl)
            P.op("act", lambda e, sl=sl: e.activation(out=YT[sl], in_=XT[sl], func=AF.Square, accum_out=st[:, 0:1]),
                 reads=[R_xt[sl]], writes=[R_yt[sl], R_stf], after=barF)
            P.op("act", lambda e: e.activation(out=st[:, 1:2], in_=st[:, 0:1], func=AF.Sqrt, scale=1.0 / D, bias=epsb[:, 0:1]),
                 reads=[R_stf, R_const], writes=[R_stf])
            P.op("dve", lambda e: e.reciprocal(out=st[:, 2:3], in_=st[:, 1:2]), reads=[R_stf], writes=[R_stf])
            P.op("dve", lambda e, sl=sl: e.scalar_tensor_tensor(out=YT[sl], in0=XT[sl], scalar=st[:, 2:3], in1=FN, op0=ALU.mult, op1=ALU.mult),
                 reads=[R_xt[sl], R_stf], writes=[R_yt[sl]], after=[o_fn])
            P.op("sp", lambda e, i=i, sl=sl: e.dma_start(out=y[i * 128:(i + 1) * 128, :], in_=YT[sl]), reads=[R_yt[sl]], dma="oy%d" % sl)

        return locals()

    rec = []
    plan(Prog(), None, rec)
    P = Prog()
    plan(P, rec, None)
    keys = sorted(P.dmacount.keys())
    P.out_keys = [k for k in keys]
    P.lastval = {}
    for eng in P.ENG:
        for o in P.ops[eng]:
            if o.isdma:
                P.lastval[o.semkey] = max(P.lastval.get(o.semkey, 0), o.val)
    sems = {}
    for k in ["pe", "act", "dve", "pool"] + keys:
        sems[k] = es.enter_context(nc.semaphore("s_" + k))
    with nc.Block() as block:
        P.emit(nc, sems, block)
    es.close()
    return nc


def _rope_tables(half):
    pos = np.arange(NS) + half * NS
    row = (pos // 64).astype(np.float32)
    col = (pos % 64).astype(np.float32)
    inv = (10000.0 ** (-np.arange(32, dtype=np.float32) / 32)).astype(np.float32)
    ar = row[None, :] * inv[:, None]
    ac = col[None, :] * inv[:, None]
    C = np.concatenate([np.cos(ar), np.cos(ar), np.cos(ac), np.cos(ac)], 0)
    S = np.concatenate([-np.sin(ar), np.sin(ar), -np.sin(ac), np.sin(ac)], 0)
    return C.astype(np.float32), S.astype(np.float32)


def _invcnt(half):
    out = np.zeros((4, T), np.float32)
    for gi, w in enumerate((2, 4, 8, 16)):
        def cnt(tpos, s):
            lo = np.clip(tpos - w // 2, 0, s)
            hi = np.clip(tpos + w - w // 2, 0, s)
            return (hi - lo).astype(np.float32)
        out[gi, 0:NS] = 1.0 / cnt(np.arange(NS) + half * NS, 2048)
        out[gi, NS:NS + 256] = 1.0 / cnt(np.arange(256), 256)
        out[gi, NS + 256:T] = 1.0 / cnt(np.arange(256), 256)
    return out


_NC_CACHE = {}


def kernel(x_prompt, x_sample, cache_k, cache_v, c, c_ctx, norm_w, w_ada, b_ada, w_in,
           q_norm_w, k_norm_w, sgu_norm_w, w_sgu, b_sgu, w_pool, pool_scale,
           w_br_a, w_br_b, w_br_c, w_merge, b_merge, w_out, final_norm_w):
    f = lambda a: np.ascontiguousarray(np.asarray(a, dtype=np.float32))
    x_prompt, x_sample, cache_k, cache_v = f(x_prompt), f(x_sample), f(cache_k), f(cache_v)
    c, c_ctx, norm_w, b_ada = f(c), f(c_ctx), f(norm_w), f(b_ada)
    if "nc" not in _NC_CACHE:
        _NC_CACHE["nc"] = build_nc()
    nc = _NC_CACHE["nc"]
    shared = {
        "w_ada": f(w_ada), "w_in": f(w_in), "w_merge": f(w_merge), "w_br_a": f(w_br_a), "w_br_b": f(w_br_b),
        "w_br_c": f(w_br_c), "w_out": f(w_out), "w_pool": f(w_pool),
    }
    vecs = np.zeros((2, 128, 128), np.float32)
    for l in range(2):
        vecs[l, :, 0:16] = f(norm_w)[l].reshape(16, 128).T
        vecs[l, :, 16:64] = b_ada[l].reshape(48, 128).T
        vecs[l, :, 64:112] = f(b_merge)[l].reshape(48, 128).T
        vecs[l, :, 112:116] = f(pool_scale)[l].reshape(4, 128).T
        vecs[l, :, 116] = f(q_norm_w)[l]
        vecs[l, :, 117] = f(k_norm_w)[l]
    shared["vecs"] = vecs
    shared["fnw"] = np.ascontiguousarray(np.broadcast_to(f(final_norm_w)[None, :], (128, D)))
    shared["sgn"] = np.ascontiguousarray(np.broadcast_to(f(sgu_norm_w)[:, None, :], (2, 128, 512)))
    shared["bsg"] = np.ascontiguousarray(np.broadcast_to(f(b_sgu).reshape(2, 1, 512), (2, 128, 512)))
    shared["wsT"] = np.ascontiguousarray(f(w_sgu).transpose(0, 3, 1, 2).reshape(2, 128, 512))
    shared["knw"] = np.ascontiguousarray(np.broadcast_to(f(k_norm_w)[:, None, :], (2, 128, 128)))
    cm = np.zeros((128, 256), np.float32)
    cm[:, 0:128] = np.eye(128, dtype=np.float32)
    for d in range(128):
        pd = d + 32 if (d % 64) < 32 else d - 32
        cm[pd, 128 + d] = 1.0
    shared["cmat"] = cm
    in_maps = []
    for core in range(8):
        b, half = core // 2, core % 2
        m = dict(shared)
        m["xs"] = np.ascontiguousarray(np.concatenate(
            [x_sample[b, half * NS:(half + 1) * NS], x_prompt[2 * core], x_prompt[2 * core + 1]], 0))
        m["ck"] = np.ascontiguousarray(cache_k[b].reshape(2, 256, 256))
        m["cv"] = np.ascontiguousarray(cache_v[b].reshape(2, 256, 256))
        ct = np.zeros((128, 16, 2), np.float32)
        ct[:, :, 0] = c_ctx.reshape(16, 128).T
        ct[:, :, 1] = c[b].reshape(16, 128).T
        m["cT"] = ct.reshape(128, 32)
        C, S = _rope_tables(half)
        m["ropec"], m["ropes"] = C, S
        m["invc"] = _invcnt(half)
        ml = np.zeros((128, 16), np.float32)
        ml[:, 0:8] = 1.0 if half == 1 else 0.0
        ml[:, 8:16] = 1.0 if half == 0 else 0.0
        m["mlr"] = ml
        in_maps.append(m)
    if _NC_CACHE.get('return_maps'):
        return in_maps
    res = run_bass_kernel_spmd(nc, in_maps, core_ids=list(range(8)))
    y_prompt = np.zeros((16, 256, D), np.float32)
    y_sample = np.zeros((4, 2048, D), np.float32)
    state_k = np.zeros((16, 2, 256, 2, 128), np.float32)
    state_v = np.zeros((16, 2, 256, 2, 128), np.float32)
    for core in range(8):
        r = res.results[core]
        b, half = core // 2, core % 2
        yy = np.asarray(r["y"])
        y_sample[b, half * NS:(half + 1) * NS] = yy[0:NS]
        y_prompt[2 * core] = yy[NS:NS + 256]
        y_prompt[2 * core + 1] = yy[NS + 256:T]
        skc = np.asarray(r["sk"]).reshape(2, 2, 256, 2, 128)
        svc = np.asarray(r["sv"]).reshape(2, 2, 256, 2, 128)
        for s_ in range(2):
            state_k[2 * core + s_] = skc[:, s_]
            state_v[2 * core + s_] = svc[:, s_]
    return (y_prompt, y_sample, state_k, state_v)
```

```python
import numpy as np
import ml_dtypes
import concourse.bass as bass
import concourse.mybir as mybir
from concourse.bass_utils import run_bass_kernel_spmd

F32 = mybir.dt.float32
BF16 = mybir.dt.bfloat16
AF = mybir.ActivationFunctionType
ALU = mybir.AluOpType

D = 2048
T = 1536
NT = 12
KC = 16
NS = 1024
EPS = 1e-6
ATT_SCALE = 128 ** -0.5
NKEY = 2304
PADL = 16
XCH = 2048 + 2048 + 256


class Res:
    __slots__ = ("name", "w", "r", "excl")

    def __init__(self, name, excl=False):
        self.name = name
        self.w = None
        self.r = []
        self.excl = excl


class Op:
    __slots__ = ("eng", "fn", "deps", "sig", "semkey", "val", "inc", "isdma", "clock")


class Prog:
    ENG = ("pe", "act", "dve", "pool", "sp")

    def __init__(self):
        self.ops = {e: [] for e in self.ENG}
        self.dmacount = {}
        self.clock = 0

    def op(self, eng, fn, reads=(), writes=(), after=(), dma=None, inc=16):
        o = Op()
        o.clock = self.clock
        o.eng = eng
        o.fn = fn
        o.sig = False
        o.isdma = dma is not None
        deps = set()
        excl_reads = [r for r in reads if r.excl]
        reads = [r for r in reads if not r.excl]
        writes = list(writes) + excl_reads
        for r in reads:
            if r.w is not None:
                deps.add(r.w)
        for w in writes:
            if w.w is not None:
                deps.add(w.w)
            for rd in w.r:
                deps.add(rd)
        for a in after:
            if a is not None:
                deps.add(a)
        o.deps = deps
        for d in deps:
            d.sig = True
        for r in reads:
            r.r.append(o)
        for w in writes:
            w.w = o
            w.r = []
        if dma is not None:
            o.semkey = dma
            self.dmacount[dma] = self.dmacount.get(dma, 0) + 1
            o.val = self.dmacount[dma] * inc
            o.inc = inc
            o.sig = True
        else:
            o.semkey = eng
            o.val = None
            o.inc = 1
        self.ops[eng].append(o)
        return o

    def last(self, eng):
        for o in reversed(self.ops[eng]):
            if not o.isdma:
                return o
        return None

    def barrier(self):
        return [self.last(e) for e in ("pe", "act", "dve", "pool")]

    def emit(self, nc, sems, block):
        for eng in ("pe", "act", "dve", "pool"):
            cnt = 0
            for o in self.ops[eng]:
                if (not o.isdma) and o.sig:
                    cnt += 1
                    o.val = cnt
        finals = dict((k, v) for k, v in self.dmacount.items())

        def run(e, eng):
            waited = {}
            for o in self.ops[eng]:
                need = {}
                for d in o.deps:
                    if need.get(d.semkey, 0) < d.val:
                        need[d.semkey] = d.val
                for k, v in need.items():
                    if waited.get(k, 0) < v:
                        e.wait_ge(sems[k], v)
                        waited[k] = v
                ins = o.fn(e)
                if o.sig:
                    ins.then_inc(sems[o.semkey], o.inc)
            if eng == "sp":
                for o in self.ops["sp"] + self.ops["pool"] + self.ops["act"]:
                    if o.isdma and getattr(o, "final", False):
                        pass
                for k in sorted(self.out_keys):
                    e.wait_ge(sems[k], self.lastval[k])

        @block.tensor
        def _(e):
            run(e, "pe")

        @block.scalar
        def _(e):
            run(e, "act")

        @block.vector
        def _(e):
            run(e, "dve")

        @block.gpsimd
        def _(e):
            run(e, "pool")

        @block.sync
        def _(e):
            run(e, "sp")


def build_nc(groups=None, nlayers=2, stop=None):
    if groups is None:
        groups = [[0, 1], [2, 3], [4, 5], [6, 7]]
    nc = bass.Bass("TRN2", target_bir_lowering=False)
    _memo = {}
    import contextlib
    es = contextlib.ExitStack()

    def _mem(key, fn):
        if key not in _memo:
            _memo[key] = fn()
        return _memo[key]

    def plan(P, wplan, rec):

        def din(name, shape, dt=F32):
            return _mem("d_" + name, lambda: nc.dram_tensor(name, list(shape), dt, kind="ExternalInput").ap())

        def dout(name, shape):
            return _mem("d_" + name, lambda: nc.dram_tensor(name, list(shape), F32, kind="ExternalOutput").ap())

        xs = din("xs", [T, D])
        w_ada = din("w_ada", [2, D, 6144])
        w_in = din("w_in", [2, D, 5120])
        w_merge = din("w_merge", [2, D, 6144])
        w_br_a = din("w_br_a", [2, 1024, D])
        w_br_b = din("w_br_b", [2, 512, D])
        w_br_c = din("w_br_c", [2, 512, D])
        w_out = din("w_out", [2, D, D])
        w_pool = din("w_pool", [2, 4, 128, 128])
        ck = din("ck", [2, 256, 256])
        cv = din("cv", [2, 256, 256])
        vecs = din("vecs", [2, 128, 128])
        cT = din("cT", [128, 32])
        fnw = din("fnw", [128, D])
        sgn = din("sgn", [2, 128, 512])
        bsg = din("bsg", [2, 128, 512])
        wsT = din("wsT", [2, 128, 512])
        knw = din("knw", [2, 128, 128])
        ropec = din("ropec", [128, NS])
        ropes = din("ropes", [128, NS])
        invc = din("invc", [4, T])
        mlr = din("mlr", [128, 16])
        cmat = din("cmat", [128, 256])

        y = dout("y", [T, D])
        sk = dout("sk", [2, 512, 256])
        sv = dout("sv", [2, 512, 256])

        x1 = _mem("x1", lambda: nc.dram_tensor("x1", [T, D], F32).ap())
        x2 = _mem("x2", lambda: nc.dram_tensor("x2", [T, D], F32).ap())
        xch_in = [_mem("xi%d" % l, lambda l=l: nc.dram_tensor("xch_in%d" % l, [128, XCH], BF16).ap()) for l in range(2)]
        xch_out = [_mem("xo%d" % l, lambda l=l: nc.dram_tensor("xch_out%d" % l, [256, XCH], BF16).ap()) for l in range(2)]


        def sb(name, shape, dt):
            return _mem("s_" + name, lambda: es.enter_context(nc.sbuf_tensor(name, list(shape), dt)))

        hT = sb("hT", [128, KC * T], BF16)
        qT = sb("qT", [128, 8 * T], BF16)
        regA = sb("regA", [128, 12288], BF16)
        regM = sb("regM", [128, 12288], F32)
        regMb = regM[:].bitcast(BF16)
        wbuf = sb("wbuf", [128, 6 * 2048], BF16)
        gbc = sb("gbc", [128, 2 * D], BF16)
        cst = sb("cst", [128, 4 * 128], BF16)
        vec = sb("vec", [128, 128], F32)
        AB = sb("AB", [128, 128], F32)
        rc = sb("rc", [128, NS], BF16)
        rs = sb("rs", [128, NS], BF16)
        sgn_t = sb("sgn_t", [128, 512], F32)
        bsg_t = sb("bsg_t", [128, 512], F32)
        wsT_t = sb("wsT_t", [128, 512], BF16)
        knw_t = sb("knw_t", [128, 128], F32)
        wp_t = sb("wp_t", [128, 512], BF16)
        mlr_t = sb("mlr_t", [128, 16], F32)
        epsb = sb("epsb", [128, 1], F32)
        st = sb("st", [128, 16], F32)
        t6 = sb("t6", [128, 7 * 512], F32)
        t6b = sb("t6b", [128, 2 * 512], BF16)

        ps = [_mem("ps%d" % i, lambda i=i: es.enter_context(nc.psum_tensor("ps%d" % i, [128, 512], F32))) for i in range(8)]
        R_ps = [Res("ps%d" % i, excl=True) for i in range(8)]

        ident = cst[:, 0:128]
        perm = cst[:, 128:256]
        onesm = cst[:, 256:384]
        ones = cst[:, 384:512]

        R_h = [Res("h%d" % i) for i in range(NT)]
        R_q = [[Res("q%d_%d" % (h, c)) for c in range(3)] for h in range(8)]
        R_regA = Res("regA")
        R_kTs = Res("kTs")
        R_Vs = Res("Vs")
        R_kTp = [Res("kTp%d" % g) for g in range(2)]
        R_Vp = [Res("Vp%d" % i) for i in range(4)]
        R_bg = [[Res("bg") for c in range(3)] for g in range(4)]
        R_cg = [[Res("cg") for c in range(3)] for g in range(4)]
        R_m = [[Res("m") for c in range(3)] for m in range(16)]
        R_w = [Res("w%d" % i) for i in range(14)]
        R_gbc = Res("gbc")
        R_cst = Res("cst")
        R_vec = Res("vec")
        R_AB = Res("AB")
        R_misc = Res("misc")
        R_const = Res("const")
        R_sgn, R_bsg, R_wsT, R_knw, R_wp = Res("sgn"), Res("bsg"), Res("wsT"), Res("knw"), Res("wp")
        R_stk, R_stv, R_stf = Res("stk"), Res("stv"), Res("stf")
        R_x1 = [Res("x1_%d" % i) for i in range(NT)]
        R_x2 = [Res("x2_%d" % i) for i in range(NT)]
        R_xchin = [Res("xi"), Res("xi")]
        R_xchout = [Res("xo"), Res("xo")]

        def hv(kc, t0, n):
            return hT[:, kc * T + t0: kc * T + t0 + n]

        def qv(h, t0, n):
            return qT[:, h * T + t0: h * T + t0 + n]

        def kTs(g, j0, n):
            return regA[:, g * NKEY + j0: g * NKEY + j0 + n]

        def Vs(tile, g):
            return regA[:, 4608 + tile * 256 + g * 128: 4608 + tile * 256 + g * 128 + 128]

        def kTp(g, j0, n):
            return regA[:, 9216 + g * 512 + j0: 9216 + g * 512 + j0 + n]

        def Vp(tile, g):
            return regA[:, 10240 + tile * 256 + g * 128: 10240 + tile * 256 + g * 128 + 128]

        def bgv(g, t0, n):
            return regA[:, g * T + t0: g * T + t0 + n]

        def cgv(g, t0, n):
            return regA[:, 6144 + g * T + t0: 6144 + g * T + t0 + n]

        def mv(m, t0, n):
            return regMb[:, m * T + t0: m * T + t0 + n]

        def tiles_of(t0, n):
            return list(range(t0 // 128, (t0 + n + 127) // 128))

        NHS = 14

        def wloc(hs):
            return (wbuf, hs * 2048) if hs < 6 else (hT, (hs - 6) * 2048)

        def wv(hs, kc_i, ncols, c0, n):
            t_, o_ = wloc(hs)
            base = o_ + kc_i * ncols + c0
            return t_[:, base: base + n]

        wemitted = {}

        def _emit_load(spec, after):
            hs = spec["hs"]
            d, s_ = spec["dst"], spec["src"]
            if spec["kind"] == "main":
                res = [R_w[hs + i] for i in range(spec["nhalf"])]
                return P.op("pool", lambda e, d=d, s_=s_: e.dma_start(out=d, in_=s_), writes=res, after=after, dma="w%d" % hs)
            o = P.op("pool", lambda e, d=d, s_=s_: e.dma_start(out=d, in_=s_), dma="w%d" % hs)
            R_w[hs].w = o
            return o

        def _cover(sp):
            return range(sp["hs"], sp["hs"] + sp.get("nhalf", 1))

        def _load(spec, after):
            k = state["wk"]
            state["wk"] += 1
            P.clock = k + 1
            spec["late"] = bool(after)
            if wplan is None:
                spec["ready"] = max([o.clock for h in _cover(spec) for o in R_w[h].r] + [0])
                rec.append(spec)
                return _emit_load(spec, after)
            if k not in wemitted:
                wemitted[k] = _emit_load(spec, after)
            kk = k + 1
            if kk < len(wplan) and kk not in wemitted and not wplan[kk]["late"] and wplan[kk]["kind"] == "main" \
                    and wplan[kk]["ready"] <= k:
                wemitted[kk] = _emit_load(wplan[kk], ())
            return wemitted[k]

        def load_w(hs, nhalf, src2d, kc, ncols, kc_off=0, total_cols=None, after=()):
            tc_ = ncols if total_cols is None else total_cols
            t_, o_ = wloc(hs)
            base = o_ + kc_off * tc_
            dst = t_[:, base: base + kc * tc_].rearrange("p (k n) -> p k n", k=kc)
            src = src2d.rearrange("(k p) n -> p k n", p=128)
            return _load({"kind": "main", "hs": hs, "nhalf": nhalf, "dst": dst, "src": src}, tuple(after))

        def load_w_extra(hs, src2d, kc, ncols, kc_off):
            t_, o_ = wloc(hs)
            base = o_ + kc_off * ncols
            dst = t_[:, base: base + kc * ncols].rearrange("p (k n) -> p k n", k=kc)
            src = src2d.rearrange("(k p) n -> p k n", p=128)
            return _load({"kind": "extra", "hs": hs, "dst": dst, "src": src}, ())

        def mm_group(out_ap, pairs, reads, writes, after=()):
            def fn(e, pairs=pairs, out_ap=out_ap):
                n = len(pairs)
                ins = None
                for i, (l, r) in enumerate(pairs):
                    ins = e.matmul(out_ap, l, r, start=(i == 0), stop=(i == n - 1))
                return ins
            return P.op("pe", fn, reads, writes, after)

        XT = [regM[:, 0:2048], regM[:, 2048:4096]]
        R_xt = [Res("xt0"), Res("xt1")]
        XN = regM[:, 4096:5120].bitcast(BF16)
        R_xn = Res("xn")
        SQJ = regM[:, 5120:6144].bitcast(BF16)
        R_sqj = Res("sqj")
        TF = [regM[:, 6144 + i * 512: 6144 + (i + 1) * 512] for i in range(6)]
        R_tf = [Res("tf%d" % i) for i in range(6)]
        TB = [regM[:, 9216 + i * 256: 9216 + (i + 1) * 256].bitcast(BF16) for i in range(6)]
        R_tb = [Res("tb%d" % i) for i in range(6)]
        SQT = [SQJ[:, i * 512:(i + 1) * 512] for i in range(4)]
        R_sqt = [Res("sqt%d" % i) for i in range(4)]
        TFX = [regM[:, 2304 + i * 512: 2304 + (i + 1) * 512] for i in range(2)]
        R_tfx = [Res("tfx%d" % i) for i in range(2)]
        QTB = [(TB[0], R_tb[0]), (TB[1], R_tb[1]), (TB[2], R_tb[2]), (TB[3], R_tb[3]),
               (SQT[0], R_sqt[0]), (SQT[1], R_sqt[1]), (SQT[2], R_sqt[2]), (SQT[3], R_sqt[3])]
        QTF = [(TF[0], R_tf[0]), (TF[3], R_tf[3]), (TF[1], R_tf[1]), (TF[4], R_tf[4]),
               (TF[2], R_tf[2]), (TF[5], R_tf[5]), (TFX[0], R_tfx[0]), (TFX[1], R_tfx[1])]
        SCB = regM[:, 6144:6144 + 16].bitcast(BF16)
        SCF = regM[:, 6144 + 64:6144 + 96]
        SREP = [regM[:, 0:1024].bitcast(BF16), regM[:, 1024:2048].bitcast(BF16)]

        o_c = P.op("pool", lambda e: e.dma_start(out=cst[:, 0:256], in_=cmat), writes=[R_cst], dma="c0")
        P.op("dve", lambda e: e.memset(cst[:, 256:384], 1.0 / 128), writes=[R_const])
        P.op("dve", lambda e: e.memset(cst[:, 384:512], 1.0), writes=[R_const])
        P.op("dve", lambda e: e.memset(epsb[:], EPS), writes=[R_const])
        P.op("pool", lambda e: e.dma_start(out=rc[:], in_=ropec), writes=[R_const], dma="c1")
        P.op("pool", lambda e: e.dma_start(out=rs[:], in_=ropes), writes=[R_const], dma="c2")
        P.op("sp", lambda e: e.dma_start(out=mlr_t[:], in_=mlr), writes=[R_const], dma="c3")
        R_cT = Res("cT")
        CTF = regM[:, 11000:11032]
        P.op("sp", lambda e: e.dma_start(out=CTF, in_=cT), writes=[R_cT], dma="c4")

        sems_needed = set()
        state = {"wrot": 0, "last_p7_pe": None, "qkcnt": 0, "wk": 0}

        def next_slot_full():
            s = state["wrot"] % 3
            state["wrot"] += 1
            return s * 2

        x_src = [xs, x1]
        x_dst = [x1, x2]
        R_xsrc = [[Res("xs") for i in range(NT)], R_x1]
        R_xdst = [R_x1, R_x2]
        psrot = {"a": 0, "b": 0}

        def bank_main():
            b = psrot["a"] % 4
            psrot["a"] += 1
            return b

        def bank_aux():
            b = 4 + psrot["b"] % 4
            psrot["b"] += 1
            return b

        mvec = sb("mvec", [128, 128], F32)
        scf = sb("scf", [128, 32], F32)
        scb = sb("scb", [128, 32], BF16)
        st2 = sb("st2", [128, 8], F32)
        ssq = sb("ssq", [128, 72], F32)
        rsf = sb("rsf", [128, 36], F32)
        R_ssq, R_rsf, R_jk = Res("ssq"), Res("rsf"), Res("jk")
        R_mvec = Res("mvec")
        R_sc = Res("sc")
        R_st2 = [Res("st2a"), Res("st2b")]
        R_ABl = [Res("AB0"), Res("AB1")]
        R_rep = [Res("rep%d" % i) for i in range(8)]
        XN1 = regM[:, 11040:12064].bitcast(BF16)
        XNs = [XN, XN1]
        R_xns = [R_xn, Res("xn1")]
        for l_ in range(2):
            P.op("sp", lambda e, l_=l_: e.dma_start(out=mvec[:, l_ * 64:(l_ + 1) * 64], in_=vecs[l_][:, 0:64]), writes=[R_mvec], dma="mv%d" % l_)
        P.op("act", lambda e: e.activation(out=scf[:], in_=CTF, func=AF.Silu), reads=[R_cT], writes=[R_sc])
        P.op("dve", lambda e: e.tensor_copy(out=scb[:], in_=scf[:]), reads=[R_sc], writes=[R_sc])
        BM = 3

        def mod_steps(L, part):
            steps = []
            if part == 'ss':
                nblk, colbase, wbase = 16, 0, 0
            else:
                nblk, colbase, wbase = 8, 64, 4096
            first = [True]
            for blk in range(nblk):
                def step(blk=blk):
                    hs = next_slot_full()
                    load_w(hs, 2, w_ada[L][:, wbase + blk * 256: wbase + (blk + 1) * 256], KC, 256)
                    for jj in range(2):
                        j = blk * 2 + jj
                        pairs = [(wv(hs, kc, 256, jj * 128, 128), scb[:, kc * 2: kc * 2 + 2]) for kc in range(KC)]
                        mm_group(ps[BM][:, colbase + j * 2: colbase + j * 2 + 2], pairs, reads=[R_w[hs], R_w[hs + 1], R_sc],
                                 writes=[R_ps[BM]] if first[0] else [])
                        first[0] = False
                steps.append(step)

            def fin():
                lastpe = P.last("pe")
                ao = L * 64
                if part == 'ss':
                    MODT3 = TF[4][:, 0:64].rearrange("p (j s) -> p j s", s=2)
                    PS3 = ps[BM][:, 0:64].rearrange("p (j s) -> p j s", s=2)
                    for s_ in range(2):
                        P.op("dve", lambda e, s_=s_: e.tensor_tensor(out=MODT3[:, :, s_], in0=PS3[:, :, s_], in1=mvec[:, ao + 16: ao + 48], op=ALU.add),
                             reads=[R_ps[BM], R_mvec], writes=[R_tf[4]], after=[lastpe])
                    for s_ in range(2):
                        P.op("dve", lambda e, s_=s_: e.scalar_tensor_tensor(
                            out=AB[:, ao + s_ * 16: ao + (s_ + 1) * 16], in0=MODT3[:, 16:32, s_], scalar=1.0,
                            in1=mvec[:, ao: ao + 16], op0=ALU.add, op1=ALU.mult), reads=[R_tf[4], R_mvec], writes=[R_ABl[L]])
                        P.op("dve", lambda e, s_=s_: e.tensor_copy(out=AB[:, ao + 32 + s_ * 16: ao + 32 + (s_ + 1) * 16], in_=MODT3[:, 0:16, s_]),
                             reads=[R_tf[4]], writes=[R_ABl[L]])
                else:
                    GT = TF[4][:, 64:96]
                    GT3 = GT.rearrange("p (j s) -> p j s", s=2)
                    PS3 = ps[BM][:, 64:96].rearrange("p (j s) -> p j s", s=2)
                    for s_ in range(2):
                        P.op("dve", lambda e, s_=s_: e.tensor_tensor(out=GT3[:, :, s_], in0=PS3[:, :, s_], in1=mvec[:, ao + 48: ao + 64], op=ALU.add),
                             reads=[R_ps[BM], R_mvec], writes=[R_tf[4]], after=[lastpe])
                    REP = TF[3].bitcast(BF16)
                    cnt = 0
                    for s_ in range(2):
                        for q4 in range(4):
                            lp = None
                            for k4 in range(4):
                                kc = q4 * 4 + k4
                                ri = cnt % 8
                                rep = REP[:, ri * 128:(ri + 1) * 128]
                                P.op("dve", lambda e, rep=rep, kc=kc, s_=s_: e.tensor_scalar(
                                    out=rep, in0=ones, scalar1=GT[:, kc * 2 + s_: kc * 2 + s_ + 1], scalar2=None, op0=ALU.mult),
                                    reads=[R_tf[4], R_const], writes=[R_rep[ri]] + ([R_tf[3]] if cnt < 8 else []))
                                lp = P.op("pe", lambda e, rep=rep, k4=k4: e.matmul(ps[BM][:, k4 * 128:(k4 + 1) * 128], rep, ident, start=True, stop=True),
                                          reads=[R_rep[ri], R_cst, R_tf[3]], writes=[R_ps[BM]] if k4 == 0 else [])
                                cnt += 1
                            P.op("act", lambda e, s_=s_, q4=q4: e.activation(out=gbc[:, s_ * D + q4 * 512: s_ * D + (q4 + 1) * 512], in_=ps[BM][:], func=AF.Copy),
                                 reads=[R_ps[BM]], writes=[R_gbc], after=[lp])
            steps.append(fin)
            return steps

        for l in range(nlayers):
            P.op("sp", lambda e, l=l: e.dma_start(out=vec[:], in_=vecs[l]), writes=[R_vec], dma="v0")
            P.op("sp", lambda e, l=l: e.dma_start(out=sgn_t[:], in_=sgn[l]), writes=[R_sgn], dma="v1")
            P.op("sp", lambda e, l=l: e.dma_start(out=bsg_t[:], in_=bsg[l]), writes=[R_bsg], dma="v2")
            P.op("pool", lambda e, l=l: e.dma_start(out=wsT_t[:], in_=wsT[l]), writes=[R_wsT], dma="v3")
            P.op("sp", lambda e, l=l: e.dma_start(out=knw_t[:], in_=knw[l]), writes=[R_knw], dma="v4")
            P.op("pool", lambda e, l=l: e.dma_start(
                out=wp_t[:].rearrange("p (g n) -> p g n", g=4),
                in_=w_pool[l].rearrange("g p n -> p g n")), writes=[R_wp], dma="v5")

            if l == 0:
                pending = mod_steps(0, 'ss')
            else:
                pending = []
            for i in range(NT):
                sl = i % 2
                so = sl * 4
                P.op("sp", lambda e, l=l, i=i, sl=sl: e.dma_start(out=XT[sl], in_=x_src[l][i * 128:(i + 1) * 128, :]),
                     reads=[R_xsrc[l][i]], writes=[R_xt[sl]], dma="xt%d" % sl)
                P.op("act", lambda e, sl=sl, so=so: e.activation(out=SQJ, in_=XT[sl], func=AF.Square, accum_out=st2[:, so:so + 1]),
                     reads=[R_xt[sl]], writes=[R_sqj, R_st2[sl]])
                P.op("act", lambda e, so=so: e.activation(out=st2[:, so + 1:so + 2], in_=st2[:, so:so + 1], func=AF.Sqrt, scale=1.0 / D, bias=epsb[:, 0:1]),
                     reads=[R_const], writes=[R_st2[sl]])
                P.op("dve", lambda e, so=so: e.reciprocal(out=st2[:, so + 2:so + 3], in_=st2[:, so + 1:so + 2]), reads=[], writes=[R_st2[sl]])
                P.op("dve", lambda e, sl=sl, so=so: e.tensor_scalar(out=XNs[sl], in0=XT[sl], scalar1=st2[:, so + 2:so + 3], scalar2=None, op0=ALU.mult),
                     reads=[R_xt[sl], R_st2[sl]], writes=[R_xns[sl]])
                for half in range(2):
                    b = 4 + 2 * sl + half
                    pb = ps[b][:].bitcast(BF16)

                    def tr(e, half=half, pb=pb, sl=sl):
                        ins = None
                        for k8 in range(8):
                            kc = half * 8 + k8
                            ins = e.transpose(pb[:, k8 * 128:(k8 + 1) * 128], XNs[sl][:, kc * 128:(kc + 1) * 128], ident)
                        return ins
                    P.op("pe", tr, reads=[R_xns[sl], R_cst], writes=[R_ps[b]])
                    dst3 = hT[:, :].rearrange("p (k t) -> p k t", k=KC)[:, half * 8:(half + 1) * 8, i * 128:(i + 1) * 128]
                    src3 = pb.rearrange("p (k t) -> p k t", k=8)
                    if half == 0:
                        P.op("act", lambda e, dst3=dst3, src3=src3: e.activation(out=dst3, in_=src3, func=AF.Copy), reads=[R_ps[b]], writes=[R_h[i]],
                             after=[state["last_p7_pe"]])
                    else:
                        P.op("dve", lambda e, dst3=dst3, src3=src3: e.tensor_copy(out=dst3, in_=src3), reads=[R_ps[b]], writes=[R_h[i]],
                             after=[state["last_p7_pe"]])
                for _ in range(2):
                    if pending:
                        pending.pop(0)()
            while pending:
                pending.pop(0)()
            ao = l * 64
            for kc in range(KC):
                P.op("dve", lambda e, kc=kc, ao=ao: e.tensor_scalar(
                    out=hv(kc, 0, NS), in0=hv(kc, 0, NS), scalar1=AB[:, ao + 16 + kc: ao + 17 + kc], scalar2=AB[:, ao + 48 + kc: ao + 49 + kc],
                    op0=ALU.mult, op1=ALU.add), reads=[R_ABl[l]], writes=R_h[0:8])
                P.op("act", lambda e, kc=kc, ao=ao: e.activation(
                    out=hv(kc, NS, 512), in_=hv(kc, NS, 512), func=AF.Identity, scale=AB[:, ao + kc: ao + kc + 1], bias=AB[:, ao + 32 + kc: ao + 33 + kc]),
                    reads=[R_ABl[l]], writes=R_h[8:12])

            if stop == 'norm':
                break
            Rh_chunk = [[R_h[c * 4 + j] for j in range(4)] for c in range(3)]
            XSTG = regM[:, 0:XCH // 2].bitcast(BF16)
            R_xstg = Res("xstg")
            bar0 = P.barrier()

            def qk_process(b, dst_ap, dst_res, is_sample, c, wcol, ncols_tok, extra_after=()):
                n = ncols_tok
                i0 = state["qkcnt"]
                state["qkcnt"] += 1
                tb0, rb0 = QTB[(i0 % 4) * 2]
                tb1, rb1 = QTB[(i0 % 4) * 2 + 1]
                P.op("act", lambda e: e.activation(out=tb0[:, 0:n], in_=ps[b][:, 0:n], func=AF.Square),
                     reads=[R_ps[b]], writes=[rb0], after=extra_after)
                P.op("act", lambda e: e.activation(out=tb1[:, 0:n], in_=ps[b][:, 0:n], func=AF.Copy, scale=vec[:, wcol:wcol + 1]),
                     reads=[R_ps[b], R_vec], writes=[rb1])
                b2 = bank_aux()
                mm_group(ps[b2][:, 0:n], [(onesm, tb0[:, 0:n])], reads=[rb0, R_const], writes=[R_ps[b2]])
                tf0, rf0 = QTF[(i0 % 4) * 2]
                P.op("act", lambda e: e.activation(out=tf0[:, 0:n], in_=ps[b2][:, 0:n], func=AF.Sqrt, bias=epsb[:, 0:1]),
                     reads=[R_ps[b2], R_const], writes=[rf0])
                P.op("dve", lambda e: e.reciprocal(out=tf0[:, 0:n], in_=tf0[:, 0:n]), reads=[rf0], writes=[rf0])
                if is_sample:
                    b3 = bank_aux()
                    mm_group(ps[b3][:, 0:n], [(perm, tb1[:, 0:n])], reads=[rb1, R_cst], writes=[R_ps[b3]])
                    tf1, rf1 = QTF[(i0 % 4) * 2 + 1]
                    t0 = c * 512
                    P.op("pool", lambda e: e.tensor_tensor(out=tf1[:, 0:n], in0=tb1[:, 0:n], in1=rc[:, t0:t0 + n], op=ALU.mult),
                         reads=[rb1, R_const], writes=[rf1])
                    P.op("dve", lambda e: e.tensor_tensor(out=tb0[:, 0:n], in0=ps[b3][:, 0:n], in1=rs[:, t0:t0 + n], op=ALU.mult),
                         reads=[R_ps[b3], R_const], writes=[rb0])
                    P.op("pool", lambda e: e.tensor_tensor(out=tf1[:, 0:n], in0=tf1[:, 0:n], in1=tb0[:, 0:n], op=ALU.add),
                         reads=[rb0, rf1], writes=[rf1])
                    P.op("dve", lambda e: e.tensor_tensor(out=dst_ap, in0=tf1[:, 0:n], in1=tf0[:, 0:n], op=ALU.mult),
                         reads=[rf1, rf0], writes=[dst_res])
                else:
                    P.op("dve", lambda e: e.tensor_tensor(out=dst_ap, in0=tb1[:, 0:n], in1=tf0[:, 0:n], op=ALU.mult),
                         reads=[rb1, rf0], writes=[dst_res])

            hs = next_slot_full()
            load_w(hs, 2, w_in[l][:, 1024:1280], KC, 256)
            hs_k = hs
            for g in range(2):
                for c in range(3):
                    b = bank_main()
                    pairs = [(wv(hs, kc, 256, g * 128, 128), hv(kc, c * 512, 512)) for kc in range(KC)]
                    mm_group(ps[b][:], pairs, reads=[R_w[hs], R_w[hs + 1]] + Rh_chunk[c], writes=[R_ps[b]])
                    if c < 2:
                        dst = XSTG[:, g * 1024 + c * 512: g * 1024 + (c + 1) * 512]
                        qk_process(b, dst, R_xstg, True, c, 117, 512, extra_after=bar0)
                    else:
                        qk_process(b, kTp(g, 0, 512), R_kTp[g], False, c, 117, 512, extra_after=bar0 + [None])
            hs_v = next_slot_full()
            load_w(hs_v, 2, w_in[l][:, 1280:1536], KC, 256)
            OT = TF
            for i in range(NT):
                b = bank_main()
                pairs = [(hv(kc, i * 128, 128), wv(hs_v, kc, 256, 0, 256)) for kc in range(KC)]
                mm_group(ps[b][:, 0:256], pairs, reads=[R_w[hs_v], R_w[hs_v + 1], R_h[i]], writes=[R_ps[b]])
                if i < 8:
                    P.op("act", lambda e, b=b, i=i: e.activation(out=XSTG[:, 2048 + i * 256: 2048 + (i + 1) * 256], in_=ps[b][:, 0:256],
                                                               func=AF.Copy), reads=[R_ps[b]], writes=[R_xstg])
                else:
                    ip = i - 8
                    P.op("act", lambda e, b=b, ip=ip: e.activation(out=regA[:, 10240 + ip * 256: 10240 + (ip + 1) * 256], in_=ps[b][:, 0:256],
                                                                 func=AF.Copy), reads=[R_ps[b]], writes=[R_Vp[ip]])
                    tfi = ip % 2
                    P.op("dve", lambda e, b=b, tfi=tfi: e.tensor_copy(out=TF[tfi][:, 0:256], in_=ps[b][:, 0:256]),
                         reads=[R_ps[b]], writes=[R_tf[tfi]])
                    P.op("sp", lambda e, l=l, ip=ip, tfi=tfi: e.dma_start(out=sv[l][ip * 128:(ip + 1) * 128, :], in_=TF[tfi][:, 0:256]),
                         reads=[R_tf[tfi]], dma="osv%d" % tfi)
                    b2 = bank_main()
                    pairs = [(hv(kc, i * 128, 128), wv(hs_k, kc, 256, 0, 256)) for kc in range(KC)]
                    mm_group(ps[b2][:, 0:256], pairs, reads=[R_w[hs_k], R_w[hs_k + 1], R_h[i]], writes=[R_ps[b2]])
                    tk = 2 + ip % 2
                    for g in range(2):
                        P.op("act", lambda e, b2=b2, g=g: e.activation(out=TB[4][:, 0:128], in_=ps[b2][:, g * 128:(g + 1) * 128], func=AF.Square,
                                                                     accum_out=st[:, 4 + g: 5 + g]), reads=[R_ps[b2]], writes=[R_tb[4], R_stk])
                    P.op("act", lambda e: e.activation(out=st[:, 6:8], in_=st[:, 4:6], func=AF.Sqrt, scale=1.0 / 128, bias=epsb[:, 0:1]),
                         reads=[R_stk, R_const], writes=[R_stk])
                    P.op("dve", lambda e: e.reciprocal(out=st[:, 8:10], in_=st[:, 6:8]), reads=[R_stk], writes=[R_stk])
                    for g in range(2):
                        P.op("dve", lambda e, b2=b2, g=g, tk=tk: e.scalar_tensor_tensor(
                            out=TF[tk][:, g * 128:(g + 1) * 128], in0=ps[b2][:, g * 128:(g + 1) * 128], scalar=st[:, 8 + g: 9 + g],
                            in1=knw_t[:], op0=ALU.mult, op1=ALU.mult), reads=[R_ps[b2], R_stk, R_knw], writes=[R_tf[tk]])
                    P.op("sp", lambda e, l=l, ip=ip, tk=tk: e.dma_start(out=sk[l][ip * 128:(ip + 1) * 128, :], in_=TF[tk][:, 0:256]),
                         reads=[R_tf[tk]], dma="osk%d" % (tk - 2))
            for kc in range(KC):
                pass
            P.op("pool", lambda e: e.tensor_copy(
                out=XSTG[:, 4096:4352].rearrange("p (k t) -> p k t", k=KC)[:, :, 0:8],
                in_=hT[:, :].rearrange("p (k t) -> p k t", k=KC)[:, :, 0:8]), reads=[R_h[0]], writes=[R_xstg], after=bar0)
            P.op("pool", lambda e: e.tensor_copy(
                out=XSTG[:, 4096:4352].rearrange("p (k t) -> p k t", k=KC)[:, :, 8:16],
                in_=hT[:, :].rearrange("p (k t) -> p k t", k=KC)[:, :, NS - 8:NS]), reads=[R_h[7]], writes=[R_xstg])
            P.op("pool", lambda e, l=l: e.dma_start(out=xch_in[l][:, :], in_=XSTG), reads=[R_xstg], writes=[R_xchin[l]], dma="xi")
            P.op("pool", lambda e, l=l: e.collective_compute(
                "AllGather", ALU.bypass, replica_groups=groups,
                ins=[xch_in[l][:, :]], outs=[xch_out[l][:, :]]), reads=[R_xchin[l]], writes=[R_xchout[l]], dma="cc", inc=1)
            HH = TB[5][:, 0:256]
            n_sp_before = len(P.ops["sp"])
            R_hh = R_tb[5]
            for r in range(2):
                P.op("sp", lambda e, l=l, r=r: e.dma_start(
                    out=regA[:, 0:4608].rearrange("p (g j) -> p g j", g=2)[:, :, r * 1024:(r + 1) * 1024],
                    in_=xch_out[l][r * 128:(r + 1) * 128, 0:2048].rearrange("p (g j) -> p g j", g=2)),
                    reads=[R_xchout[l]], writes=[R_kTs] if r == 0 else [], after=bar0, dma="gk%d" % r)
                P.op("sp", lambda e, l=l, r=r: e.dma_start(
                    out=regA[:, 4608 + r * 2048: 4608 + (r + 1) * 2048], in_=xch_out[l][r * 128:(r + 1) * 128, 2048:4096]),
                    reads=[R_xchout[l]], writes=[R_Vs] if r == 0 else [], after=bar0, dma="gv%d" % r)
            P.op("sp", lambda e, l=l: e.dma_start(
                out=HH.rearrange("p (k t) -> p k t", k=KC)[:, :, 0:8],
                in_=xch_out[l][0:128, 4096:4352].rearrange("p (k t) -> p k t", k=KC)[:, :, 8:16]),
                reads=[R_xchout[l]], writes=[R_hh], dma="gh0")
            o_hh = P.op("sp", lambda e, l=l: e.dma_start(
                out=HH.rearrange("p (k t) -> p k t", k=KC)[:, :, 8:16],
                in_=xch_out[l][128:256, 4096:4352].rearrange("p (k t) -> p k t", k=KC)[:, :, 0:8]),
                reads=[R_xchout[l]], dma="gh1")
            gath_ops = [o for o in P.ops["sp"][n_sp_before:] if o.isdma]
            o_cv = P.op("pool", lambda e, l=l: e.dma_start(
                out=regA[:, 4608 + 16 * 256: 4608 + 18 * 256].rearrange("p (t n) -> p t n", t=2),
                in_=cv[l].rearrange("(t p) n -> p t n", p=128)), after=bar0, dma="cv")
            CKT = TB[4]
            P.op("pool", lambda e, l=l: e.dma_start(out=CKT.rearrange("p (t n) -> p t n", t=2),
                                                   in_=ck[l].rearrange("(t p) n -> p t n", p=128)), writes=[R_tb[4]], dma="ck")
            bck = 7
            pbk = ps[bck][:].bitcast(BF16)

            def trk(e):
                ins = None
                for t_ in range(2):
                    for g in range(2):
                        ins = e.transpose(pbk[:, (t_ * 2 + g) * 128:(t_ * 2 + g + 1) * 128],
                                          CKT[:, t_ * 256 + g * 128: t_ * 256 + (g + 1) * 128], ident)
                return ins
            P.op("pe", trk, reads=[R_tb[4], R_cst], writes=[R_ps[bck]])
            o_ckw = None
            for t_ in range(2):
                for g in range(2):
                    o_ckw = P.op("dve", lambda e, t_=t_, g=g: e.tensor_copy(
                        out=kTs(g, 2048 + t_ * 128, 128), in_=pbk[:, (t_ * 2 + g) * 128:(t_ * 2 + g + 1) * 128]),
                        reads=[R_ps[bck]], after=bar0)
            for qb in range(4):
                hs = next_slot_full()
                load_w(hs, 2, w_in[l][:, qb * 256:(qb + 1) * 256], KC, 256)
                for hh in range(2):
                    h = qb * 2 + hh
                    for c in range(3):
                        b = bank_main()
                        pairs = [(wv(hs, kc, 256, hh * 128, 128), hv(kc, c * 512, 512)) for kc in range(KC)]
                        mm_group(ps[b][:], pairs, reads=[R_w[hs], R_w[hs + 1]] + Rh_chunk[c], writes=[R_ps[b]])
                        qk_process(b, qv(h, c * 512, 512), R_q[h][c], c < 2, c, 116, 512)

            if stop == 'qkv':
                break
            def attention(h, qt0, nq, keys, resq, extra_reads, extra_after):
                g = h // 4
                par = h % 2
                bo, bd = (4, 5) if par == 0 else (6, 7)
                nk = len(keys)
                sbanks = [0, 1, 2]
                e_ops = []

                def s_mm(kt):
                    b = sbanks[kt % 3]
                    return mm_group(ps[b][:, 0:nq], [(keys[kt][0], qv(h, qt0, nq))], reads=[resq] + extra_reads,
                                    writes=[R_ps[b]], after=extra_after)

                def e_act(kt):
                    b = sbanks[kt % 3]
                    tb = TB[kt % 4]
                    return P.op("act", lambda e, b=b, tb=tb: e.activation(out=tb[:, 0:nq], in_=ps[b][:, 0:nq], func=AF.Exp, scale=ATT_SCALE),
                                reads=[R_ps[b]], writes=[R_tb[kt % 4]])

                def pv_mm(kt):
                    tb = TB[kt % 4]

                    def fn(e, kt=kt, tb=tb):
                        e.matmul(ps[bo][:, 0:nq], keys[kt][1], tb[:, 0:nq], start=(kt == 0), stop=(kt == nk - 1))
                        return e.matmul(ps[bd][:, 0:nq], ones, tb[:, 0:nq], start=(kt == 0), stop=(kt == nk - 1))
                    wr = [R_ps[bo], R_ps[bd]] if kt == 0 else []
                    return P.op("pe", fn, reads=[R_tb[kt % 4], R_const] + extra_reads, writes=wr)

                s_mm(0)
                e_act(0)
                if nk > 1:
                    s_mm(1)
                    e_act(1)
                lastpv = None
                for kt in range(nk):
                    lastpv = pv_mm(kt)
                    if kt + 2 < nk:
                        s_mm(kt + 2)
                        e_act(kt + 2)
                tfr = TF[par]
                P.op("dve", lambda e: e.reciprocal(out=tfr[:, 0:nq], in_=ps[bd][:, 0:nq]), reads=[R_ps[bd]], writes=[R_tf[par]], after=[lastpv])
                P.op("dve", lambda e: e.tensor_tensor(out=qv(h, qt0, nq), in0=ps[bo][:, 0:nq], in1=tfr[:, 0:nq], op=ALU.mult),
                     reads=[R_ps[bo], R_tf[par]], writes=[resq], after=[lastpv])

            pending = mod_steps(l, 'gate') + (mod_steps(l + 1, 'ss') if l + 1 < nlayers else [])
            for s_ in range(2):
                for h in range(8):
                    g = h // 4
                    keys = [(kTp(g, s_ * 256 + t_ * 128, 128), Vp(s_ * 2 + t_, g)) for t_ in range(2)]
                    attention(h, 1024 + s_ * 256, 256, keys, R_q[h][2], [R_kTp[g], R_Vp[s_ * 2], R_Vp[s_ * 2 + 1]], [])
            gdeps = [P.ops["sp"][-1]]
            exch_after = gath_ops + [o_cv, o_ckw]
            for c in range(2):
                for h in range(8):
                    g = h // 4
                    keys = [(kTs(g, kt * 128, 128), Vs(kt, g)) for kt in range(18)]
                    attention(h, c * 512, 512, keys, R_q[h][c], [R_kTs, R_Vs], exch_after)
                    for _ in range(2):
                        if pending:
                            pending.pop(0)()
            while pending:
                pending.pop(0)()
            last_att_pe = P.last("pe")

            if stop == 'att':
                break
            for gb_ in range(4):
                hs = next_slot_full()
                load_w(hs, 2, w_in[l][:, 1536 + gb_ * 256: 1536 + (gb_ + 1) * 256], KC, 256)
                for hh in range(2):
                    h = gb_ * 2 + hh
                    for c in range(3):
                        b = bank_main()
                        pairs = [(wv(hs, kc, 256, hh * 128, 128), hv(kc, c * 512, 512)) for kc in range(KC)]
                        mm_group(ps[b][:], pairs, reads=[R_w[hs], R_w[hs + 1]] + Rh_chunk[c], writes=[R_ps[b]])
                        tb = TB[(h * 3 + c) % 4]
                        rtb = R_tb[(h * 3 + c) % 4]
                        P.op("act", lambda e, b=b, tb=tb: e.activation(out=tb[:], in_=ps[b][:], func=AF.Silu), reads=[R_ps[b]], writes=[rtb])
                        P.op("dve", lambda e, tb=tb, h=h, c=c: e.tensor_tensor(out=qv(h, c * 512, 512), in0=qv(h, c * 512, 512), in1=tb[:], op=ALU.mult),
                             reads=[rtb], writes=[R_q[h][c]])

            if stop == 'ga':
                break
            VC = regM[:, 0:3072].bitcast(BF16)
            R_vc = [Res("vc%d" % i) for i in range(NT)]
            hs_vb = []
            for half in range(2):
                hs = next_slot_full()
                load_w(hs, 2, w_in[l][:, 3072 + half * 256: 3072 + (half + 1) * 256], KC, 256)
                hs_vb.append(hs)
            for i in range(NT):
                b = bank_main()
                for half in range(2):
                    pairs = [(hv(kc, i * 128, 128), wv(hs_vb[half], kc, 256, 0, 256)) for kc in range(KC)]
                    mm_group(ps[b][:, half * 256:(half + 1) * 256], pairs, reads=[R_w[hs_vb[half]], R_w[hs_vb[half] + 1], R_h[i]],
                             writes=[R_ps[b]] if half == 0 else [])
                lpe = P.last("pe")
                P.op("act", lambda e, b=b: e.activation(out=SQJ[:, 0:512], in_=ps[b][:], func=AF.Square, accum_out=st[:, 10:11]),
                     reads=[R_ps[b]], writes=[R_sqj, R_stv], after=[lpe, last_att_pe])
                P.op("act", lambda e: e.activation(out=st[:, 11:12], in_=st[:, 10:11], func=AF.Sqrt, scale=1.0 / 512, bias=epsb[:, 0:1]),
                     reads=[R_stv, R_const], writes=[R_stv])
                P.op("dve", lambda e: e.reciprocal(out=st[:, 12:13], in_=st[:, 11:12]), reads=[R_stv], writes=[R_stv])
                P.op("dve", lambda e, b=b, i=i: e.scalar_tensor_tensor(
                    out=VC[:, i * 512:(i + 1) * 512], in0=ps[b][:], scalar=st[:, 12:13], in1=sgn_t[:], op0=ALU.mult, op1=ALU.mult),
                    reads=[R_ps[b], R_stv, R_sgn], writes=[R_vc[i]], after=[lpe, last_att_pe])
            for gp in range(2):
                hs_u = next_slot_full()
                load_w(hs_u, 2, w_in[l][:, 2560 + gp * 256: 2560 + (gp + 1) * 256], KC, 256)
                hs_g = next_slot_full()
                load_w(hs_g, 2, w_in[l][:, 3584 + gp * 256: 3584 + (gp + 1) * 256], KC, 256)
                for gg in range(2):
                    g = gp * 2 + gg
                    for c in range(3):
                        bu = bank_main()
                        mm_group(ps[bu][:], [(wv(hs_u, kc, 256, gg * 128, 128), hv(kc, c * 512, 512)) for kc in range(KC)],
                                 reads=[R_w[hs_u], R_w[hs_u + 1]] + Rh_chunk[c], writes=[R_ps[bu]])
                        bg_ = bank_main()
                        mm_group(ps[bg_][:], [(wv(hs_g, kc, 256, gg * 128, 128), hv(kc, c * 512, 512)) for kc in range(KC)],
                                 reads=[R_w[hs_g], R_w[hs_g + 1]] + Rh_chunk[c], writes=[R_ps[bg_]])
                        bm = bank_aux()

                        def mix(e, g=g, c=c, bm=bm):
                            ins = None
                            for j in range(4):
                                i = c * 4 + j
                                ins = e.matmul(ps[bm][:, j * 128:(j + 1) * 128], VC[:, i * 512 + g * 128: i * 512 + (g + 1) * 128],
                                               wsT_t[:, g * 128:(g + 1) * 128], start=True, stop=True)
                            return ins
                        P.op("pe", mix, reads=[R_vc[c * 4 + j] for j in range(4)] + [R_wsT], writes=[R_ps[bm]])
                        k_ = (g * 3 + c) % 2
                        tfa, tfb = TF[3 + k_ * 1], TF[5 - k_ * 0] if False else TF[4 + k_] if False else TF[3 + k_]
                        tfa = TF[3 + k_]
                        rfa = R_tf[3 + k_]
                        tbx = TB[k_]
                        rbx = R_tb[k_]
                        P.op("act", lambda e, bg_=bg_, tbx=tbx: e.activation(out=tbx[:], in_=ps[bg_][:], func=AF.Silu), reads=[R_ps[bg_]], writes=[rbx])
                        P.op("dve", lambda e, bu=bu, tbx=tbx, tfa=tfa: e.tensor_tensor(out=tfa[:], in0=ps[bu][:], in1=tbx[:], op=ALU.mult),
                             reads=[R_ps[bu], rbx], writes=[rfa])
                        tfc = TF[5]
                        for j in range(4):
                            P.op("dve", lambda e, bm=bm, g=g, tfc=tfc, j=j: e.tensor_tensor(
                                out=tfc[:, j * 128:(j + 1) * 128], in0=ps[bm][:, j * 128:(j + 1) * 128],
                                in1=bsg_t[:, g * 128:(g + 1) * 128], op=ALU.add),
                                reads=[R_ps[bm], R_bsg], writes=[R_tf[5]])
                        P.op("dve", lambda e, g=g, c=c, tfa=tfa, tfc=tfc: e.tensor_tensor(out=bgv(g, c * 512, 512), in0=tfa[:], in1=tfc[:], op=ALU.mult),
                             reads=[rfa, R_tf[5]], writes=[R_bg[g][c]], after=[last_att_pe])

            if stop == 'B':
                break
            ZW = PADL + 1024 + 8 + PADL + 256 + 8 + PADL + 256 + 8
            ZA = regM[:, 0:ZW]
            ZB = regM[:, ZW: 2 * ZW]
            last_5a_pe = P.last("pe")
            R_za, R_zb = Res("za"), Res("zb")
            segs = [(0, 1024, 0), (1024, 256, PADL + 1024 + 8), (1280, 256, 2 * PADL + 1024 + 8 + 256 + 8)]
            wins = [2, 4, 8, 16]
            IV = TF[2]
            for gp in range(2):
                hs_z = next_slot_full()
                load_w(hs_z, 2, w_in[l][:, 4096 + gp * 256: 4096 + (gp + 1) * 256], KC, 256)
                hs_c = next_slot_full()
                load_w(hs_c, 2, w_in[l][:, 4608 + gp * 256: 4608 + (gp + 1) * 256], KC, 256)
                for gg in range(2):
                    g = gp * 2 + gg
                    w_ = wins[g]
                    P.op("pool", lambda e: e.memset(ZA, 0.0), writes=[R_za], after=[last_5a_pe])
                    bh = bank_aux()
                    mm_group(ps[bh][:, 0:16], [(wv(hs_z, kc, 256, gg * 128, 128), HH[:, kc * 16:(kc + 1) * 16]) for kc in range(KC)],
                             reads=[R_w[hs_z], R_w[hs_z + 1], R_hh], writes=[R_ps[bh]], after=[o_hh])
                    P.op("dve", lambda e, bh=bh: e.tensor_tensor(out=ZA[:, PADL - 8:PADL], in0=ps[bh][:, 0:8], in1=mlr_t[:, 0:8], op=ALU.mult),
                         reads=[R_ps[bh], R_const], writes=[R_za])
                    P.op("dve", lambda e, bh=bh: e.tensor_tensor(out=ZA[:, PADL + 1024:PADL + 1032], in0=ps[bh][:, 8:16], in1=mlr_t[:, 8:16], op=ALU.mult),
                         reads=[R_ps[bh], R_const], writes=[R_za])
                    for c in range(3):
                        bz = bank_main()
                        mm_group(ps[bz][:], [(wv(hs_z, kc, 256, gg * 128, 128), hv(kc, c * 512, 512)) for kc in range(KC)],
                                 reads=[R_w[hs_z], R_w[hs_z + 1]] + Rh_chunk[c], writes=[R_ps[bz]])
                        if c < 2:
                            P.op("act", lambda e, bz=bz, c=c: e.activation(out=ZA[:, PADL + c * 512: PADL + (c + 1) * 512], in_=ps[bz][:], func=AF.Copy),
                                 reads=[R_ps[bz]], writes=[R_za])
                        else:
                            for s2 in range(2):
                                eo = segs[1 + s2][2] + PADL
                                P.op("act", lambda e, bz=bz, s2=s2, eo=eo: e.activation(out=ZA[:, eo:eo + 256], in_=ps[bz][:, s2 * 256:(s2 + 1) * 256], func=AF.Copy),
                                     reads=[R_ps[bz]], writes=[R_za])
                    cur, curR, oth, othR = ZA, R_za, ZB, R_zb
                    zsrc = ZA
                    steps = []
                    s_ = 1
                    while s_ < w_:
                        steps.append(s_)
                        s_ *= 2
                    ZC = regM[:, 2 * ZW: 3 * ZW]
                    R_zc = Res("zc")
                    bufs = [(ZB, R_zb), (ZC, R_zc)]
                    src, srcR = ZA, R_za
                    for si, s_ in enumerate(steps):
                        dst, dstR = bufs[si % 2]
                        P.op("pool", lambda e, dst=dst, src=src, s_=s_: e.tensor_tensor(out=dst[:, s_:ZW], in0=src[:, s_:ZW], in1=src[:, 0:ZW - s_], op=ALU.add),
                             reads=[srcR], writes=[dstR])
                        P.op("pool", lambda e, dst=dst, src=src, s_=s_: e.tensor_copy(out=dst[:, 0:s_], in_=src[:, 0:s_]), reads=[srcR], writes=[dstR])
                        src, srcR = dst, dstR
                    bsh = w_ // 2 - 1
                    for c in range(3):
                        P.op("sp", lambda e, g=g, c=c: e.dma_start(out=IV[:], in_=invc[g:g + 1, c * 512:(c + 1) * 512].partition_broadcast(128)),
                             writes=[R_tf[2]], dma="iv")
                        tfd = TF[0]
                        tbd = TB[2]
                        if c < 2:
                            eo = PADL + c * 512
                            P.op("dve", lambda e, src=src, eo=eo, tfd=tfd, bsh=bsh: e.tensor_tensor(out=tfd[:], in0=src[:, eo + bsh: eo + bsh + 512], in1=IV[:], op=ALU.mult),
                                 reads=[srcR, R_tf[2]], writes=[R_tf[0]])
                            P.op("dve", lambda e, eo=eo, tfd=tfd, tbd=tbd: e.tensor_tensor(out=tbd[:], in0=tfd[:], in1=ZA[:, eo:eo + 512], op=ALU.subtract),
                                 reads=[R_tf[0], R_za], writes=[R_tb[2]])
                        else:
                            for s2 in range(2):
                                eo = segs[1 + s2][2] + PADL
                                P.op("dve", lambda e, src=src, eo=eo, s2=s2, tfd=tfd, bsh=bsh: e.tensor_tensor(
                                    out=tfd[:, s2 * 256:(s2 + 1) * 256], in0=src[:, eo + bsh: eo + bsh + 256], in1=IV[:, s2 * 256:(s2 + 1) * 256], op=ALU.mult),
                                    reads=[srcR, R_tf[2]], writes=[R_tf[0]])
                                P.op("dve", lambda e, eo=eo, s2=s2, tfd=tfd, tbd=tbd: e.tensor_tensor(
                                    out=tbd[:, s2 * 256:(s2 + 1) * 256], in0=tfd[:, s2 * 256:(s2 + 1) * 256], in1=ZA[:, eo:eo + 256], op=ALU.subtract),
                                    reads=[R_tf[0], R_za], writes=[R_tb[2]])
                        bp = bank_aux()
                        mm_group(ps[bp][:], [(wp_t[:, g * 128:(g + 1) * 128], tbd[:])], reads=[R_tb[2], R_wp], writes=[R_ps[bp]])
                        bc = bank_main()
                        mm_group(ps[bc][:], [(wv(hs_c, kc, 256, gg * 128, 128), hv(kc, c * 512, 512)) for kc in range(KC)],
                                 reads=[R_w[hs_c], R_w[hs_c + 1]] + Rh_chunk[c], writes=[R_ps[bc]])
                        tbs = TB[3]
                        P.op("act", lambda e, bc=bc, tbs=tbs: e.activation(out=tbs[:], in_=ps[bc][:], func=AF.Silu), reads=[R_ps[bc]], writes=[R_tb[3]])
                        P.op("dve", lambda e, bp=bp, g=g, c=c, tbs=tbs: e.scalar_tensor_tensor(
                            out=cgv(g, c * 512, 512), in0=ps[bp][:], scalar=vec[:, 112 + g: 113 + g], in1=tbs[:], op0=ALU.mult, op1=ALU.mult),
                            reads=[R_ps[bp], R_tb[3], R_vec], writes=[R_cg[g][c]], after=[last_att_pe])

            if stop == 'C':
                break
            bar6 = P.barrier()
            ACC = [t6[:, i * 512:(i + 1) * 512] for i in range(3)]
            R_acc = [Res("acc%d" % i) for i in range(3)]
            TMP = [t6[:, (3 + i) * 512:(4 + i) * 512] for i in range(2)]
            R_tmp = [Res("tmp%d" % i) for i in range(2)]
            SG = [t6b[:, i * 512:(i + 1) * 512] for i in range(2)]
            R_sg = [Res("sg%d" % i) for i in range(2)]
            cnt6 = 0
            brsrc = [w_br_a, w_br_b, w_br_c]
            for m in range(16):
                hsb = 4 + m % 2
                load_w(hsb, 1, w_br_a[l][:, m * 128:(m + 1) * 128], 8, 128, kc_off=0)
                load_w_extra(hsb, w_br_b[l][:, m * 128:(m + 1) * 128], 4, 128, kc_off=8)
                o_wabc = load_w_extra(hsb, w_br_c[l][:, m * 128:(m + 1) * 128], 4, 128, kc_off=12)
                for br in range(3):
                    hsg = (m * 3 + br) % 4
                    load_w(hsg, 1, w_merge[l][:, br * D + m * 128: br * D + (m + 1) * 128], KC, 128)
                    for c in range(3):
                        bgm = bank_main()
                        mm_group(ps[bgm][:], [(wv(hsg, kc, 128, 0, 128), hv(kc, c * 512, 512)) for kc in range(KC)],
                                 reads=[R_w[hsg]] + Rh_chunk[c], writes=[R_ps[bgm]])
                        bbr = bank_main()
                        if br == 0:
                            pairs = [(wv(hsb, kc, 128, 0, 128), qv(kc, c * 512, 512)) for kc in range(8)]
                            rr = [R_q[kc][c] for kc in range(8)]
                        elif br == 1:
                            pairs = [(wv(hsb, 8 + kc, 128, 0, 128), bgv(kc, c * 512, 512)) for kc in range(4)]
                            rr = [R_bg[kc][c] for kc in range(4)]
                        else:
                            pairs = [(wv(hsb, 12 + kc, 128, 0, 128), cgv(kc, c * 512, 512)) for kc in range(4)]
                            rr = [R_cg[kc][c] for kc in range(4)]
                        mm_group(ps[bbr][:], pairs, reads=[R_w[hsb]] + rr, writes=[R_ps[bbr]], after=[o_wabc])
                        k_ = cnt6 % 2
                        cnt6 += 1
                        P.op("act", lambda e, bgm=bgm, k_=k_, br=br, m=m: e.activation(
                            out=SG[k_][:], in_=ps[bgm][:], func=AF.Sigmoid, bias=vec[:, 64 + br * 16 + m: 65 + br * 16 + m]),
                            reads=[R_ps[bgm], R_vec], writes=[R_sg[k_]])
                        if br == 0:
                            P.op("dve", lambda e, bbr=bbr, k_=k_, c=c: e.tensor_tensor(out=ACC[c][:], in0=ps[bbr][:], in1=SG[k_][:], op=ALU.mult),
                                 reads=[R_ps[bbr], R_sg[k_]], writes=[R_acc[c]])
                        else:
                            P.op("dve", lambda e, bbr=bbr, k_=k_: e.tensor_tensor(out=TMP[k_][:], in0=ps[bbr][:], in1=SG[k_][:], op=ALU.mult),
                                 reads=[R_ps[bbr], R_sg[k_]], writes=[R_tmp[k_]])
                            if br == 1:
                                P.op("dve", lambda e, k_=k_, c=c: e.tensor_tensor(out=ACC[c][:], in0=ACC[c][:], in1=TMP[k_][:], op=ALU.add),
                                     reads=[R_tmp[k_]], writes=[R_acc[c]])
                            else:
                                P.op("dve", lambda e, k_=k_, c=c, m=m: e.tensor_tensor(out=mv(m, c * 512, 512), in0=ACC[c][:], in1=TMP[k_][:], op=ALU.add),
                                     reads=[R_tmp[k_], R_acc[c]], writes=[R_m[m][c]], after=bar6)

            if stop == 'merge':
                break
            NX7 = 3
            XC = [t6[:, i * 512:(i + 1) * 512] for i in range(NX7)]
            R_xc = [Res("xc%d" % i) for i in range(NX7)]
            XO = [t6[:, (NX7 + i) * 512:(NX7 + i + 1) * 512] for i in range(NX7)]
            R_xo = [Res("xo%d" % i) for i in range(NX7)]
            bar7 = P.barrier()
            cnt7 = 0
            last_p6_pe = P.last("pe")
            fuse_ss = (stop is None and l == nlayers - 1)
            pend_sq = []
            p7slots = [next_slot_full(), next_slot_full(), next_slot_full(), 6, 8, 10, 12]
            p7rot = 0
            for nb in range(4):
                hsp = []
                for kh in range(2):
                    hs = p7slots[p7rot % 7]
                    p7rot += 1
                    load_w(hs, 2, w_out[l][kh * 1024:(kh + 1) * 1024, nb * 512:(nb + 1) * 512], 8, 512,
                           after=[last_p6_pe] if hs >= 6 else [])
                    hsp.append(hs)
                for i in range(NT):
                    tg = i // 4
                    s_ = 1 if i < 8 else 0
                    k_ = cnt7 % NX7
                    cnt7 += 1
                    P.op("act", lambda e, l=l, i=i, nb=nb, k_=k_: e.dma_start(out=XC[k_], in_=x_src[l][i * 128:(i + 1) * 128, nb * 512:(nb + 1) * 512]),
                         reads=[R_xsrc[l][i]], writes=[R_xc[k_]], dma="xc%d" % k_, after=bar7 if cnt7 <= NX7 else [])
                    b = bank_main()
                    pairs = [(mv(kh * 8 + m, i * 128, 128), wv(hsp[kh], m, 512, 0, 512)) for kh in range(2) for m in range(8)]
                    mm_group(ps[b][:], pairs, reads=[R_w[hsp[0]], R_w[hsp[0] + 1], R_w[hsp[1]], R_w[hsp[1] + 1]] + [R_m[m][tg] for m in range(16)],
                             writes=[R_ps[b]])
                    P.op("dve", lambda e, b=b, k_=k_, s_=s_, nb=nb: e.tensor_tensor(
                        out=XO[k_], in0=ps[b][:], in1=gbc[:, s_ * D + nb * 512: s_ * D + (nb + 1) * 512], op=ALU.mult),
                        reads=[R_ps[b], R_gbc], writes=[R_xo[k_]], after=(bar7 if cnt7 <= NX7 else []))
                    P.op("dve", lambda e, k_=k_: e.tensor_tensor(out=XO[k_], in0=XO[k_], in1=XC[k_], op=ALU.add),
                         reads=[R_xc[k_]], writes=[R_xo[k_]])
                    P.op("sp", lambda e, l=l, i=i, nb=nb, k_=k_: e.dma_start(out=x_dst[l][i * 128:(i + 1) * 128, nb * 512:(nb + 1) * 512], in_=XO[k_]),
                         reads=[R_xo[k_]], writes=[R_xdst[l][i]] if nb == 3 else [], dma="xo%d" % k_)
                    if fuse_ss:
                        def sq_op(k_=k_, i=i, nb=nb):
                            P.op("act", lambda e: e.activation(out=t6b[:, 0:512], in_=XO[k_], func=AF.Square,
                                                               accum_out=ssq[:, i * 4 + nb: i * 4 + nb + 1]),
                                 reads=[R_xo[k_]], writes=[R_jk, R_ssq])
                        pend_sq.append(sq_op)
                        if len(pend_sq) > 1:
                            pend_sq.pop(0)()
            while pend_sq:
                pend_sq.pop(0)()
            state["last_p7_pe"] = P.last("pe")

        barF = P.barrier()
        if stop is not None:
            nlayers_f = 0
        else:
            nlayers_f = 1
        FN = regM[:, 4096:6144]
        if nlayers_f:
            P.op("sp", lambda e: e.dma_start(out=FN, in_=fnw), writes=[R_sqj, R_xn], after=barF, dma="fn")
            o_fn = P.ops["sp"][-1]
            S3 = ssq[:, 0:48].rearrange("p (i n) -> p i n", n=4)
            T3 = ssq[:, 48:72].rearrange("p (i n) -> p i n", n=2)
            P.op("dve", lambda e: e.tensor_tensor(out=T3, in0=S3[:, :, 0:2], in1=S3[:, :, 2:4], op=ALU.add), reads=[R_ssq], writes=[R_ssq])
            P.op("dve", lambda e: e.tensor_tensor(out=rsf[:, 0:12], in0=T3[:, :, 0], in1=T3[:, :, 1], op=ALU.add), reads=[R_ssq], writes=[R_rsf])
            P.op("act", lambda e: e.activation(out=rsf[:, 12:24], in_=rsf[:, 0:12], func=AF.Sqrt, scale=1.0 / D, bias=epsb[:, 0:1]),
                 reads=[R_rsf, R_const], writes=[R_rsf])
            P.op("dve", lambda e: e.reciprocal(out=rsf[:, 24:36], in_=rsf[:, 12:24]), reads=[R_rsf], writes=[R_rsf])
            FB = [regM[:, 0:2048], regM[:, 2048:4096], regM[:, 6144:8192], regM[:, 8192:10240]]
            R_fb = [Res("fb%d" % i) for i in range(4)]
            for i in range(NT):
                sl = i % 4
                P.op("sp", lambda e, i=i, sl=sl: e.dma_start(out=FB[sl], in_=x_dst[nlayers - 1][i * 128:(i + 1) * 128, :]),
                     reads=[R_xdst[nlayers - 1][i]], writes=[R_fb[sl]], after=barF, dma="fb%d" % sl)
                P.op("dve", lambda e, sl=sl, i=i: e.scalar_tensor_tensor(out=FB[sl], in0=FB[sl], scalar=rsf[:, 24 + i: 25 + i], in1=FN,
                                                                         op0=ALU.mult, op1=ALU.mult),
                     reads=[R_rsf], writes=[R_fb[sl]], after=[o_fn])
                P.op("act", lambda e, i=i, sl=sl: e.dma_start(out=y[i * 128:(i + 1) * 128, :], in_=FB[sl]), reads=[R_fb[sl]], dma="oy%d" % sl)

        return locals()

    rec = []
    plan(Prog(), None, rec)
    P = Prog()
    plan(P, rec, None)
    keys = sorted(P.dmacount.keys())
    P.out_keys = [k for k in keys]
    P.lastval = {}
    for eng in P.ENG:
        for o in P.ops[eng]:
            if o.isdma:
                P.lastval[o.semkey] = max(P.lastval.get(o.semkey, 0), o.val)
    sems = {}
    for k in ["pe", "act", "dve", "pool"] + keys:
        sems[k] = es.enter_context(nc.semaphore("s_" + k))
    with nc.Block() as block:
        P.emit(nc, sems, block)
    es.close()
    return nc


def _rope_tables(half):
    pos = np.arange(NS) + half * NS
    row = (pos // 64).astype(np.float32)
    col = (pos % 64).astype(np.float32)
    inv = (10000.0 ** (-np.arange(32, dtype=np.float32) / 32)).astype(np.float32)
    ar = row[None, :] * inv[:, None]
    ac = col[None, :] * inv[:, None]
    C = np.concatenate([np.cos(ar), np.cos(ar), np.cos(ac), np.cos(ac)], 0)
    S = np.concatenate([-np.sin(ar), np.sin(ar), -np.sin(ac), np.sin(ac)], 0)
    return C.astype(np.float32), S.astype(np.float32)


def _invcnt(half):
    out = np.zeros((4, T), np.float32)
    for gi, w in enumerate((2, 4, 8, 16)):
        def cnt(tpos, s):
            lo = np.clip(tpos - w // 2, 0, s)
            hi = np.clip(tpos + w - w // 2, 0, s)
            return (hi - lo).astype(np.float32)
        out[gi, 0:NS] = 1.0 / cnt(np.arange(NS) + half * NS, 2048)
        out[gi, NS:NS + 256] = 1.0 / cnt(np.arange(256), 256)
        out[gi, NS + 256:T] = 1.0 / cnt(np.arange(256), 256)
    return out


_NC_CACHE = {}


def kernel(x_prompt, x_sample, cache_k, cache_v, c, c_ctx, norm_w, w_ada, b_ada, w_in,
           q_norm_w, k_norm_w, sgu_norm_w, w_sgu, b_sgu, w_pool, pool_scale,
           w_br_a, w_br_b, w_br_c, w_merge, b_merge, w_out, final_norm_w):
    f = lambda a: np.ascontiguousarray(np.asarray(a, dtype=np.float32))
    x_prompt, x_sample, cache_k, cache_v = f(x_prompt), f(x_sample), f(cache_k), f(cache_v)
    c, c_ctx, norm_w, b_ada = f(c), f(c_ctx), f(norm_w), f(b_ada)
    if "nc" not in _NC_CACHE:
        _NC_CACHE["nc"] = build_nc()
    nc = _NC_CACHE["nc"]
    shared = {
        "w_ada": f(w_ada), "w_in": f(w_in), "w_merge": f(w_merge), "w_br_a": f(w_br_a), "w_br_b": f(w_br_b),
        "w_br_c": f(w_br_c), "w_out": f(w_out), "w_pool": f(w_pool),
    }
    vecs = np.zeros((2, 128, 128), np.float32)
    for l in range(2):
        vecs[l, :, 0:16] = f(norm_w)[l].reshape(16, 128).T
        vecs[l, :, 16:64] = b_ada[l].reshape(48, 128).T
        vecs[l, :, 64:112] = f(b_merge)[l].reshape(48, 128).T
        vecs[l, :, 112:116] = f(pool_scale)[l].reshape(4, 128).T
        vecs[l, :, 116] = f(q_norm_w)[l]
        vecs[l, :, 117] = f(k_norm_w)[l]
    shared["vecs"] = vecs
    shared["fnw"] = np.ascontiguousarray(np.broadcast_to(f(final_norm_w)[None, :], (128, D)))
    shared["sgn"] = np.ascontiguousarray(np.broadcast_to(f(sgu_norm_w)[:, None, :], (2, 128, 512)))
    shared["bsg"] = np.ascontiguousarray(np.broadcast_to(f(b_sgu).reshape(2, 1, 512), (2, 128, 512)))
    shared["wsT"] = np.ascontiguousarray(f(w_sgu).transpose(0, 3, 1, 2).reshape(2, 128, 512))
    shared["knw"] = np.ascontiguousarray(np.broadcast_to(f(k_norm_w)[:, None, :], (2, 128, 128)))
    cm = np.zeros((128, 256), np.float32)
    cm[:, 0:128] = np.eye(128, dtype=np.float32)
    for d in range(128):
        pd = d + 32 if (d % 64) < 32 else d - 32
        cm[pd, 128 + d] = 1.0
    shared["cmat"] = cm
    in_maps = []
    for core in range(8):
        b, half = core // 2, core % 2
        m = dict(shared)
        m["xs"] = np.ascontiguousarray(np.concatenate(
            [x_sample[b, half * NS:(half + 1) * NS], x_prompt[2 * core], x_prompt[2 * core + 1]], 0))
        m["ck"] = np.ascontiguousarray(cache_k[b].reshape(2, 256, 256))
        m["cv"] = np.ascontiguousarray(cache_v[b].reshape(2, 256, 256))
        ct = np.zeros((128, 16, 2), np.float32)
        ct[:, :, 0] = c_ctx.reshape(16, 128).T
        ct[:, :, 1] = c[b].reshape(16, 128).T
        m["cT"] = ct.reshape(128, 32)
        C, S = _rope_tables(half)
        m["ropec"], m["ropes"] = C, S
        m["invc"] = _invcnt(half)
        ml = np.zeros((128, 16), np.float32)
        ml[:, 0:8] = 1.0 if half == 1 else 0.0
        ml[:, 8:16] = 1.0 if half == 0 else 0.0
        m["mlr"] = ml
        in_maps.append(m)
    if _NC_CACHE.get('return_maps'):
        return in_maps
    res = run_bass_kernel_spmd(nc, in_maps, core_ids=list(range(8)))
    y_prompt = np.zeros((16, 256, D), np.float32)
    y_sample = np.zeros((4, 2048, D), np.float32)
    state_k = np.zeros((16, 2, 256, 2, 128), np.float32)
    state_v = np.zeros((16, 2, 256, 2, 128), np.float32)
    for core in range(8):
        r = res.results[core]
        b, half = core // 2, core % 2
        yy = np.asarray(r["y"])
        y_sample[b, half * NS:(half + 1) * NS] = yy[0:NS]
        y_prompt[2 * core] = yy[NS:NS + 256]
        y_prompt[2 * core + 1] = yy[NS + 256:T]
        skc = np.asarray(r["sk"]).reshape(2, 2, 256, 2, 128)
        svc = np.asarray(r["sv"]).reshape(2, 2, 256, 2, 128)
        for s_ in range(2):
            state_k[2 * core + s_] = skc[:, s_]
            state_v[2 * core + s_] = svc[:, s_]
    return (y_prompt, y_sample, state_k, state_v)
```
